# Optimizing a Trainium2 kernel written in Bass

```python
import math
import jax, jax.numpy as jnp
from jax import lax
import numpy as np

D_MODEL = 1024
BATCH = 4
SEQ = 4096
DEPTH = 4

CHUNK = 64
N_META = 16
SB_BLOCK = 128
HEAD_DIM = 64
SB_HEADS = D_MODEL // 128
RW_HEADS = D_MODEL // 128
C_SB = SB_HEADS * HEAD_DIM
C_RW = RW_HEADS * HEAD_DIM
W_LORA = 64
A_LORA = 64
V_LORA = 32
G_LORA = 160
RW_COLS = 3 * C_RW + W_LORA + A_LORA + G_LORA
IN_COLS = 3 * C_SB + RW_COLS + 2 * D_MODEL
FFN_HIDDEN = ((8 * D_MODEL // 3 + 255) // 256) * 256
RMS_EPS = 1e-6
GN_EPS = 64e-5

kernel_name = "hybrid_stickbreaking_rwkv7_gated_trunk"


def rms_norm(x, g):
    xf = x.astype(jnp.float32)
    y = xf * lax.rsqrt(jnp.mean(xf * xf, axis=-1, keepdims=True) + RMS_EPS)
    return (y * g.astype(jnp.float32)).astype(x.dtype)


def split_heads(t, n_heads):
    b, l, _ = t.shape
    return t.reshape(b, l, n_heads, HEAD_DIM)


def token_shift(p, mu):
    prev = jnp.pad(p, ((0, 0), (1, 0), (0, 0)))[:, :-1]
    return p + (prev - p) * mu


def stick_breaking_attention(q, k, v):
    b, l, h, dh = q.shape
    n_blocks = l // SB_BLOCK
    q_blocks = q.reshape(b, n_blocks, SB_BLOCK, h, dh).transpose(1, 0, 2, 3, 4)
    k_pos = jnp.arange(l)
    scale = dh ** -0.5

    def one_block(args):
        i, q_blk = args
        z = jnp.einsum("bqhd,bkhd->bhqk", q_blk, k).astype(jnp.float32) * scale
        q_pos = i * SB_BLOCK + jnp.arange(SB_BLOCK)
        mask = k_pos[None, :] < q_pos[:, None]
        neg_log_one_minus_beta = jnp.where(mask, jax.nn.softplus(z), 0.0)
        between = lax.cumsum(neg_log_one_minus_beta, axis=3, reverse=True) - neg_log_one_minus_beta
        log_a = jax.nn.log_sigmoid(z) - between
        a = jnp.where(mask, jnp.exp(log_a), 0.0)
        return jnp.einsum("bhqk,bkhd->bqhd", a.astype(v.dtype), v)

    out = lax.map(one_block, (jnp.arange(n_blocks), q_blocks))
    return out.transpose(1, 0, 2, 3, 4).reshape(b, l, h, dh)


def rwkv7_recurrence(r, decay, k, v, kk, a):
    b, l, h, n = r.shape
    f32 = jnp.float32

    def step(s, inp):
        r_t, w_t, k_t, v_t, kk_t, a_t = inp
        s_kk = jnp.einsum("bhvk,bhk->bhv", s, -kk_t)
        s = (s * w_t[:, :, None, :]
             + s_kk[..., None] * (kk_t * a_t)[:, :, None, :]
             + v_t[..., None] * k_t[:, :, None, :])
        y = jnp.einsum("bhvk,bhk->bhv", s, r_t)
        return s, y

    xs = tuple(jnp.moveaxis(t.astype(f32), 1, 0) for t in (r, decay, k, v, kk, a))
    s0 = jnp.zeros((b, h, n, n), f32)
    _, ys = lax.scan(step, s0, xs)
    return jnp.moveaxis(ys, 0, 1)


def group_norm_heads(y, w, b):
    mean = jnp.mean(y, axis=-1, keepdims=True)
    var = jnp.mean(jnp.square(y - mean), axis=-1, keepdims=True)
    yn = (y - mean) * lax.rsqrt(var + GN_EPS)
    bb, l, h, n = y.shape
    return yn.reshape(bb, l, h * n) * w.astype(jnp.float32) + b.astype(jnp.float32)


def setup_inputs(seed: int = 0) -> dict:
    key = jax.random.key(seed)
    ks = jax.random.split(key, 32)
    f32 = jnp.float32
    n_vres = DEPTH - 1

    def nrm(k, shape, scale):
        return jax.random.normal(k, shape, f32) * scale

    def gain(k, shape):
        return 1.0 + 0.05 * jax.random.normal(k, shape, f32)

    return {
        "x": nrm(ks[0], (BATCH, SEQ, D_MODEL), 1.0),
        "meta_tokens": nrm(ks[1], (N_META, D_MODEL), 1.0),
        "norm_mix": gain(ks[2], (DEPTH, D_MODEL)),
        "norm_ffn": gain(ks[3], (DEPTH, D_MODEL)),
        "norm_final": gain(ks[4], (D_MODEL,)),
        "w_in": nrm(ks[5], (DEPTH, D_MODEL, IN_COLS), D_MODEL ** -0.5),
        "mu_rw": jax.random.uniform(ks[6], (DEPTH, RW_COLS), f32, 0.2, 0.8),
        "w0": jax.random.uniform(ks[7], (DEPTH, C_RW), f32, -6.5, -1.5),
        "w_up": nrm(ks[8], (DEPTH, W_LORA, C_RW), W_LORA ** -0.5),
        "a0": nrm(ks[9], (DEPTH, C_RW), 0.1),
        "a_up": nrm(ks[10], (DEPTH, A_LORA, C_RW), A_LORA ** -0.5),
        "g_up": nrm(ks[11], (DEPTH, G_LORA, C_RW), G_LORA ** -0.5),
        "k_k": 0.85 + 0.05 * jax.random.normal(ks[12], (DEPTH, C_RW), f32),
        "k_a": gain(ks[13], (DEPTH, C_RW)),
        "r_k": nrm(ks[14], (DEPTH, RW_HEADS, HEAD_DIM), 0.1),
        "ln_x_w": gain(ks[15], (DEPTH, C_RW)),
        "ln_x_b": nrm(ks[16], (DEPTH, C_RW), 0.02),
        "vres_down": nrm(ks[17], (n_vres, D_MODEL, V_LORA), D_MODEL ** -0.5),
        "vres_mu": jax.random.uniform(ks[18], (n_vres, V_LORA), f32, 0.2, 0.8),
        "vres_up": nrm(ks[19], (n_vres, V_LORA, C_RW), V_LORA ** -0.5),
        "vres0": gain(ks[20], (n_vres, C_RW)),
        "w_sb_out": nrm(ks[21], (DEPTH, C_SB, D_MODEL), C_SB ** -0.5),
        "w_rw_out": nrm(ks[22], (DEPTH, C_RW, D_MODEL), C_RW ** -0.5),
        "w_out": nrm(ks[23], (DEPTH, D_MODEL, D_MODEL), D_MODEL ** -0.5),
        "w_ffn_in": nrm(ks[24], (DEPTH, D_MODEL, 2 * FFN_HIDDEN), D_MODEL ** -0.5),
        "w_ffn_out": nrm(ks[25], (DEPTH, FFN_HIDDEN, D_MODEL), FFN_HIDDEN ** -0.5),
    }


def reference(x, meta_tokens, norm_mix, norm_ffn, norm_final, w_in, mu_rw, w0, w_up, a0, a_up,
              g_up, k_k, k_a, r_k, ln_x_w, ln_x_b, vres_down, vres_mu, vres_up, vres0,
              w_sb_out, w_rw_out, w_out, w_ffn_in, w_ffn_out):
    b, s, d = x.shape
    l_real = N_META + s
    l_pad = -(-l_real // SB_BLOCK) * SB_BLOCK
    meta = jnp.broadcast_to(meta_tokens.astype(x.dtype)[None], (b, N_META, d))
    h = jnp.concatenate([meta, x], axis=1)
    h = jnp.pad(h, ((0, 0), (0, l_pad - l_real), (0, 0)))

    in_splits = [C_SB, 2 * C_SB, 3 * C_SB, 3 * C_SB + RW_COLS, 3 * C_SB + RW_COLS + D_MODEL]
    rw_splits = [C_RW, 2 * C_RW, 3 * C_RW, 3 * C_RW + W_LORA, 3 * C_RW + W_LORA + A_LORA]
    v_first = None
    for layer in range(DEPTH):
        xn = rms_norm(h, norm_mix[layer])
        proj = xn @ w_in[layer]
        q_sb, k_sb, v_sb, rw, gate_sb, gate_rw = jnp.split(proj, in_splits, axis=-1)

        y_sb = stick_breaking_attention(split_heads(q_sb, SB_HEADS), split_heads(k_sb, SB_HEADS),
                                        split_heads(v_sb, SB_HEADS)).reshape(b, l_pad, C_SB)

        rw = token_shift(rw, mu_rw[layer])
        r, kr, vr, w_lo, a_lo, g_lo = jnp.split(rw, rw_splits, axis=-1)
        w_log = -jax.nn.softplus(-(w0[layer] + jnp.tanh(w_lo) @ w_up[layer])) - 0.5
        decay = jnp.exp(-jnp.exp(w_log.astype(jnp.float32)))
        a = jax.nn.sigmoid(a0[layer] + a_lo @ a_up[layer])
        g = jax.nn.sigmoid(g_lo) @ g_up[layer]
        if layer == 0:
            v_first = vr
        else:
            v_lo = token_shift(xn @ vres_down[layer - 1], vres_mu[layer - 1])
            vr = vr + (v_first - vr) * jax.nn.sigmoid(vres0[layer - 1] + v_lo @ vres_up[layer - 1])
        kk = split_heads((kr * k_k[layer]).astype(jnp.float32), RW_HEADS)
        kk = kk / jnp.maximum(jnp.linalg.norm(kk, axis=-1, keepdims=True), 1e-12)
        kr = kr * (1.0 + (a - 1.0) * k_a[layer])
        r_h, k_h, v_h, a_h = (split_heads(t, RW_HEADS) for t in (r, kr, vr, a))
        y_rw = rwkv7_recurrence(r_h, split_heads(decay, RW_HEADS), k_h, v_h, kk, a_h)
        y_rw = group_norm_heads(y_rw, ln_x_w[layer], ln_x_b[layer])
        bonus = jnp.sum(r_h * k_h * r_k[layer], axis=-1, keepdims=True) * v_h
        y_rw = ((y_rw + bonus.reshape(b, l_pad, C_RW).astype(jnp.float32)).astype(h.dtype)) * g

        merged = (jax.nn.sigmoid(gate_sb) * (y_sb @ w_sb_out[layer])
                  + jax.nn.sigmoid(gate_rw) * (y_rw @ w_rw_out[layer]))
        h = h + merged @ w_out[layer]

        hn = rms_norm(h, norm_ffn[layer])
        gate_ffn, up_ffn = jnp.split(hn @ w_ffn_in[layer], [FFN_HIDDEN], axis=-1)
        h = h + (jax.nn.silu(gate_ffn) * up_ffn) @ w_ffn_out[layer]

    h = rms_norm(h, norm_final)
    return h[:, N_META:N_META + s]
```

```python
import numpy as np
import concourse.bass as bass
import concourse.mybir as mybir
from concourse.bass_utils import run_bass_kernel_spmd
from contextlib import ExitStack

F32 = mybir.dt.float32
F32R = mybir.dt.float32r
AF = mybir.ActivationFunctionType
ALU = mybir.AluOpType
AX = mybir.AxisListType
NDS = 12
SAME_ENGINE_SYNC = True

D = 1024
N_META = 16
C_SB = 512
C_RW = 512
RW_COLS = 1824
IN_COLS = 5408
FFN = 2816
RMS_EPS = 1e-6
GN_EPS = 64e-5
C0 = float(np.exp(-0.5))
O_MU, O_W0, O_A0, O_KK, O_KA, O_RK, O_LNW, O_LNB, O_VMU, O_VR0 = 0, 1824, 2336, 2848, 3360, 3872, 4384, 4896, 5408, 5440
VEC_LEN = 5952


class V:
    def __init__(self, b, a):
        self.b = b
        self.a = a

    def __getitem__(self, idx):
        return V(self.b, self.a[idx])

    def re(self, pat, **kw):
        return V(self.b, self.a.rearrange(pat, **kw))


class Buf:
    def __init__(self, name, t=None):
        self.name = name
        self.t = t
        self.w = None
        self.r = {}
        self.alias = []

    def __getitem__(self, idx):
        return V(self, self.t[idx])


def R(v):
    return V(v.b, v.a.bitcast(F32R))


def _bufs(*vs):
    out = []
    for v in vs:
        if isinstance(v, V) and v.b not in out:
            out.append(v.b)
    return out


def _a(v):
    return v.a if isinstance(v, V) else v


class Sched:
    def __init__(self, nc, es):
        self.nc = nc
        self.es = es
        self.eng = {'pe': nc.tensor, 'act': nc.scalar, 'dve': nc.vector, 'pool': nc.gpsimd, 'sp': nc.sync}
        self.sem = {k: es.enter_context(nc.semaphore("s_" + k)) for k in self.eng}
        self.cnt = {k: 0 for k in self.eng}
        self.known = {k: {} for k in self.eng}
        self.dsem = [es.enter_context(nc.semaphore("d%d" % i)) for i in range(NDS)]
        self.dcnt = [0] * NDS
        self.dnext = 0
        self.nins = 0
        self.evq = 0

    def sb(self, name, shape, es=None):
        self.uid = getattr(self, 'uid', 0) + 1
        name = "%s_%d" % (name, self.uid)
        t = (es or self.es).enter_context(self.nc.sbuf_tensor(name, list(shape), F32))
        return Buf(name, t)

    def ps(self, name, es=None):
        t = (es or self.es).enter_context(self.nc.psum_tensor(name, [128, 512], F32))
        return Buf(name, t)

    def _wait(self, e, tok):
        sem, val, key = tok
        if val <= 0 or self.known[e].get(key, 0) >= val:
            return
        if key == e and (e == 'pe' or not SAME_ENGINE_SYNC):
            return
        self.eng[e].wait_ge(sem, val)
        self.known[e][key] = val
        self.nins += 1

    def _deps(self, e, reads, writes):
        for b in reads:
            if b.w is not None:
                self._wait(e, b.w)
        for b in writes:
            for bb in [b] + b.alias:
                if bb.w is not None:
                    self._wait(e, bb.w)
                for t in list(bb.r.values()):
                    self._wait(e, t)

    def _mark(self, tok, reads, writes):
        for b in reads:
            if b not in writes:
                b.r[tok[2]] = tok
        for b in writes:
            b.w = tok
            b.r = {}

    def op(self, e, meth, reads, writes, *args, **kw):
        self._deps(e, reads, writes)
        ins = getattr(self.eng[e], meth)(*args, **kw)
        self.cnt[e] += 1
        ins.then_inc(self.sem[e], 1)
        self.nins += 1
        self._mark((self.sem[e], self.cnt[e], e), reads, writes)

    def dma(self, out, in_, q='sp', **kw):
        reads, writes = _bufs(in_), _bufs(out)
        self._deps(q, reads, writes)
        i = self.dnext
        self.dnext = (i + 1) % NDS
        key = ('d', i)
        self._wait(q, (self.dsem[i], self.dcnt[i], key))
        self.eng[q].dma_start(out=_a(out), in_=_a(in_), **kw).then_inc(self.dsem[i], 16)
        self.dcnt[i] += 16
        self.nins += 1
        self._mark((self.dsem[i], self.dcnt[i], key), reads, writes)

    def barrier(self):
        toks = [(self.sem[k], self.cnt[k], k) for k in self.eng if self.cnt[k] > 0]
        toks += [(self.dsem[i], self.dcnt[i], ('d', i)) for i in range(NDS) if self.dcnt[i] > 0]
        for e in self.eng:
            for t in toks:
                if t[2] != e:
                    self._wait(e, t)

    def finish(self):
        for i in range(NDS):
            self._wait('sp', (self.dsem[i], self.dcnt[i], ('d', i)))
        for k in self.eng:
            if k != 'sp':
                self._wait('sp', (self.sem[k], self.cnt[k], k))

    def mm(self, out, lhsT, rhs, start=True, stop=True, fr=False):
        if fr:
            lhsT, rhs = R(lhsT), R(rhs)
        self.op('pe', 'matmul', _bufs(lhsT, rhs), _bufs(out), out.a, lhsT=lhsT.a, rhs=rhs.a, start=start, stop=stop)

    def tr(self, out, in_, ident):
        self.op('pe', 'transpose', _bufs(in_, ident), _bufs(out), out.a, in_.a, ident.a)

    def act(self, out, in_, func, scale=None, bias=None):
        kw = {}
        if scale is not None:
            kw['scale'] = _a(scale)
        if bias is not None:
            kw['bias'] = _a(bias)
        self.op('act', 'activation', _bufs(in_, scale, bias), _bufs(out), out=out.a, in_=in_.a, func=func, **kw)

    def tt(self, e, out, in0, in1, op):
        self.op(e, 'tensor_tensor', _bufs(in0, in1), _bufs(out), out=out.a, in0=in0.a, in1=in1.a, op=op)

    def ts(self, e, out, in0, s1, op0, s2=None, op1=None):
        kw = dict(out=out.a, in0=in0.a, scalar1=_a(s1), scalar2=_a(s2), op0=op0)
        if op1 is not None:
            kw['op1'] = op1
        self.op(e, 'tensor_scalar', _bufs(in0, s1, s2), _bufs(out), **kw)

    def stt(self, out, in0, scalar, in1, op0, op1):
        self.op('dve', 'scalar_tensor_tensor', _bufs(in0, scalar, in1), _bufs(out), out=out.a, in0=in0.a,
                scalar=_a(scalar), in1=in1.a, op0=op0, op1=op1)

    def cp(self, e, out, in_):
        if e == 'act':
            self.op('act', 'copy', _bufs(in_), _bufs(out), out=out.a, in_=in_.a)
        else:
            self.op(e, 'tensor_copy', _bufs(in_), _bufs(out), out=out.a, in_=in_.a)

    def evac(self, out, in_):
        self.evq ^= 1
        self.cp('act' if self.evq else 'dve', out, in_)

    def red(self, out, in_, op=ALU.add):
        self.op('dve', 'tensor_reduce', _bufs(in_), _bufs(out), out=out.a, in_=in_.a, axis=AX.X, op=op)

    def recip(self, out, in_):
        self.op('dve', 'reciprocal', _bufs(in_), _bufs(out), out=out.a, in_=in_.a)

    def memset(self, out, val):
        self.op('pool', 'memset', [], _bufs(out), out.a, val)

    def amask(self, buf, n, step, cm, base, cmp):
        self.memset(buf[:, 0:n], 1.0)
        self.op('pool', 'affine_select', [buf], [buf], out=buf.t[:, 0:n], in_=buf.t[:, 0:n], pattern=[[step, n]],
                compare_op=cmp, fill=0.0, base=base, channel_multiplier=cm)


def build(LP, DEPTH, debug=False):
    nc = bass.Bass("TRN2", target_bir_lowering=False)
    NT = LP // 128
    blocks = [(s, min(512, LP - s)) for s in range(0, LP, 512)]

    def din(name, shape):
        return nc.dram_tensor(name, list(shape), F32, kind="ExternalInput").ap()

    def dscr(name, shape):
        return nc.dram_tensor(name, list(shape), F32, kind=("ExternalOutput" if debug else "Internal")).ap()

    h0T = din("h0T", [D, LP])
    gains = din("gains", [128, (2 * DEPTH + 1) * 8])
    vecs = din("vecs", [DEPTH, VEC_LEN])
    w_in = din("w_in", [DEPTH, D, IN_COLS])
    wa_up = din("wa_up", [DEPTH, 128, C_RW])
    g_up = din("g_up", [DEPTH, 160, C_RW])
    vres_down = din("vres_down", [max(DEPTH - 1, 1), D, 32])
    vres_up = din("vres_up", [max(DEPTH - 1, 1), 32, C_RW])
    w_sb_out = din("w_sb_out", [DEPTH, C_SB, D])
    w_rw_out = din("w_rw_out", [DEPTH, C_RW, D])
    w_out = din("w_out", [DEPTH, D, D])
    w_ffn_in = din("w_ffn_in", [DEPTH, D, 2 * FFN])
    w_ffn_out = din("w_ffn_out", [DEPTH, FFN, D])
    outT = nc.dram_tensor("outT", [D, LP], F32, kind="ExternalOutput").ap()

    hT = dscr("hT", [D, LP])
    qT = dscr("qT", [C_SB, LP])
    kT = dscr("kT", [C_SB, LP])
    vsb = dscr("vsb", [LP, C_SB])
    rwd = dscr("rwd", [LP + 1, RW_COLS])
    vlod = dscr("vlod", [LP + 1, 32])
    gsbT = dscr("gsbT", [D, LP])
    grwT = dscr("grwT", [D, LP])
    ysbT = dscr("ysbT", [C_SB, LP])
    yrwT = dscr("yrwT", [C_RW, LP])
    vfirst = dscr("vfirst", [LP, C_RW])

    with ExitStack() as es:
        S = Sched(nc, es)
        PS = [S.ps("ps%d" % i) for i in range(8)]
        psn = [0]

        def bank():
            psn[0] = (psn[0] + 1) % 8
            return PS[psn[0]]

        ident = S.sb("ident", [128, 128])
        ones = S.sb("ones", [128, 128])
        triS = S.sb("triS", [128, 128])
        triU = S.sb("triU", [128, 128])
        triUI = S.sb("triUI", [128, 128])
        gn = S.sb("gn", [128, (2 * DEPTH + 1) * 8])
        zrow = S.sb("zrow", [1, 512])
        S.amask(ident, 128, -1, 1, 0, ALU.is_equal)
        S.memset(ones[:, :], 1.0)
        S.amask(triS, 128, -1, 1, 0, ALU.is_gt)
        S.amask(triU, 128, 1, -1, 0, ALU.is_gt)
        S.amask(triUI, 128, 1, -1, 0, ALU.is_ge)
        S.memset(zrow[:, :], 0.0)
        onesr = S.sb("onesr", [128, 128])
        triSr = S.sb("triSr", [128, 128])
        S.cp('pool', R(onesr[:, :]), ones[:, :])
        S.cp('pool', R(triSr[:, :]), triS[:, :])
        S.dma(gn[:, :], gains)
        for o in range(0, RW_COLS, 512):
            n_ = min(512, RW_COLS - o)
            S.dma(rwd[0:1, o:o + n_], zrow[:, 0:n_])
        S.dma(vlod[0:1, :], zrow[:, 0:32])

        def rowlocal(l):
            with ExitStack() as pes:
                hb = S.sb("hb", [128, 8, 512], pes)
                xn = S.sb("xn", [128, 8, 512], pes)
                wbuf = [S.sb("wbuf%d" % i, [128, 5632], pes) for i in range(2)]
                wst = [S.sb("wst%d" % i, [128, 5632], pes) for i in range(2)]
                stg = [S.sb("stg%d" % i, [128, 512], pes) for i in range(3)]
                sq = [S.sb("sq%d" % i, [128, 512], pes) for i in range(2)]
                rs = S.sb("rs", [128, 512], pes)
                if l > 0:
                    big = S.sb("big", [128, 22 * 512], pes).t

                    def sub(name, a, b):
                        return Buf(name, big[:, a * 512:b * 512].rearrange("p (c n) -> p c n", n=512))
                    actb = sub("actb", 0, 22)
                    mg, ysb, yrw = sub("mg", 0, 8), sub("ysb", 8, 12), sub("yrw", 12, 16)
                    yst = S.sb("yst", [128, 4, 512], pes)
                    actb.alias = [mg, ysb, yrw]
                    for b_ in (mg, ysb, yrw):
                        b_.alias = [actb]
                    gt = [S.sb("gt%d" % i, [128, 512], pes) for i in range(4)]
                    tmp = [S.sb("tmp%d" % i, [128, 512], pes) for i in range(4)]
                cnt = {'w': 0, 's': 0, 'g': 0, 't': 0, 'r': 0}

                def nxt(key, lst):
                    cnt[key] += 1
                    return lst[cnt[key] % len(lst)]

                def wload(src2d, kc, n):
                    cnt['w'] += 1
                    i = cnt['w'] % 2
                    vs = wst[i][:, 0:kc * n].re("p (c n) -> p c n", n=n)
                    v = wbuf[i][:, 0:kc * n].re("p (c n) -> p c n", n=n)
                    S.dma(vs, src2d.rearrange("(c p) n -> p c n", p=128))
                    S.cp(('act', 'dve', 'act', 'dve', 'act', 'pool', 'dve', 'act')[cnt['w'] % 8], R(v), vs)
                    return v

                def norm(gcol, w, rnd=True, dst=None):
                    dst = dst or xn
                    ps = bank()
                    for c in range(8):
                        s = sq[c % 2]
                        S.act(s[:, 0:w], hb[:, c, 0:w], AF.Square)
                        S.mm(ps[:, 0:w], ones[:, :], s[:, 0:w], start=(c == 0), stop=(c == 7))
                    S.ts('dve', rs[:, 0:w], ps[:, 0:w], 1.0 / D, ALU.mult, RMS_EPS, ALU.add)
                    S.act(rs[:, 0:w], rs[:, 0:w], AF.Sqrt)
                    S.recip(rs[:, 0:w], rs[:, 0:w])
                    for c in range(8):
                        xo = dst[:, c, 0:w]
                        S.stt(R(xo) if rnd else xo, hb[:, c, 0:w], gn[:, gcol + c:gcol + c + 1], rs[:, 0:w], ALU.mult, ALU.mult)

                for (s0, w) in blocks:
                    nt = w // 128
                    S.dma(hb[:, :, 0:w], (h0T if l == 0 else hT)[:, s0:s0 + w].rearrange("(c p) t -> p c t", p=128))
                    if l > 0:
                        lw_ = l - 1
                        S.dma(yst[:, :, 0:w], ysbT[:, s0:s0 + w].rearrange("(c p) t -> p c t", p=128))
                        S.cp('pool', R(ysb[:, :, 0:w]), yst[:, :, 0:w])
                        S.dma(yst[:, :, 0:w], yrwT[:, s0:s0 + w].rearrange("(c p) t -> p c t", p=128))
                        S.cp('pool', R(yrw[:, :, 0:w]), yst[:, :, 0:w])
                        wv = wload(w_sb_out[lw_], 4, 1024)
                        wv2 = wload(w_rw_out[lw_], 4, 1024)
                        for c in range(8):
                            cs = slice(c * 128, (c + 1) * 128)
                            pa = bank()
                            for kc in range(4):
                                S.mm(pa[:, 0:w], wv[:, kc, cs], ysb[:, kc, 0:w], start=(kc == 0), stop=(kc == 3), fr=True)
                            pb = bank()
                            for kc in range(4):
                                S.mm(pb[:, 0:w], wv2[:, kc, cs], yrw[:, kc, 0:w], start=(kc == 0), stop=(kc == 3), fr=True)
                            g1 = nxt('g', gt)
                            S.dma(g1[:, 0:w], gsbT[cs, s0:s0 + w])
                            g2 = nxt('g', gt)
                            S.dma(g2[:, 0:w], grwT[cs, s0:s0 + w])
                            t1 = nxt('t', tmp)
                            S.tt('dve', t1[:, 0:w], g1[:, 0:w], pa[:, 0:w], ALU.mult)
                            t2 = nxt('t', tmp)
                            S.tt('dve', t2[:, 0:w], g2[:, 0:w], pb[:, 0:w], ALU.mult)
                            S.tt('pool', R(mg[:, c, 0:w]), t2[:, 0:w], t1[:, 0:w], ALU.add)
                        for c in range(8):
                            if c % 4 == 0:
                                wv = wload(w_out[lw_][:, c * 128:(c + 4) * 128], 8, 512)
                            pa = bank()
                            for kc in range(8):
                                S.mm(pa[:, 0:w], wv[:, kc, (c % 4) * 128:(c % 4 + 1) * 128], mg[:, kc, 0:w], start=(kc == 0), stop=(kc == 7), fr=True)
                            S.tt('dve', hb[:, c, 0:w], hb[:, c, 0:w], pa[:, 0:w], ALU.add)
                        norm((DEPTH + lw_) * 8, w)
                        for j in range(22):
                            if j % 4 == 0:
                                ng = min(4, 22 - j) * 128
                                wg = wload(w_ffn_in[lw_][:, j * 128:j * 128 + ng], 8, ng)
                                wu = wload(w_ffn_in[lw_][:, FFN + j * 128:FFN + j * 128 + ng], 8, ng)
                            js = slice((j % 4) * 128, (j % 4 + 1) * 128)
                            pg = bank()
                            for kc in range(8):
                                S.mm(pg[:, 0:w], wg[:, kc, js], xn[:, kc, 0:w], start=(kc == 0), stop=(kc == 7), fr=True)
                            pu = bank()
                            for kc in range(8):
                                S.mm(pu[:, 0:w], wu[:, kc, js], xn[:, kc, 0:w], start=(kc == 0), stop=(kc == 7), fr=True)
                            t1 = nxt('t', tmp)
                            S.act(t1[:, 0:w], pg[:, 0:w], AF.Silu)
                            S.tt('dve', R(actb[:, j, 0:w]), t1[:, 0:w], pu[:, 0:w], ALU.mult)
                        for c in range(8):
                            if c % 2 == 0:
                                wv = wload(w_ffn_out[lw_][:, c * 128:(c + 2) * 128], 22, 256)
                            pa = bank()
                            for kc in range(22):
                                S.mm(pa[:, 0:w], wv[:, kc, (c % 2) * 128:(c % 2 + 1) * 128], actb[:, kc, 0:w], start=(kc == 0), stop=(kc == 21), fr=True)
                            S.tt('dve', hb[:, c, 0:w], hb[:, c, 0:w], pa[:, 0:w], ALU.add)
                    if l < DEPTH:
                        S.dma(hT[:, s0:s0 + w].rearrange("(c p) t -> p c t", p=128), hb[:, :, 0:w], q='act')
                        norm(l * 8, w)
                        groups = [('fm', 0, 512, qT, 0), ('fm', 512, 512, kT, 0), ('tm', 1024, 512, vsb, 0),
                                  ('tm', 1536, 512, rwd, 0), ('tm', 2048, 512, rwd, 512), ('tm', 2560, 512, rwd, 1024),
                                  ('tm', 3072, 288, rwd, 1536), ('sg', 3360, 512, gsbT, 0), ('sg', 3872, 512, gsbT, 512),
                                  ('sg', 4384, 512, grwT, 0), ('sg', 4896, 512, grwT, 512)]
                        if l > 0:
                            groups.append(('vd', 0, 32, vlod, 0))
                        groups2 = []
                        for (mode, c0, n, dst, d0) in groups:
                            for o in range(0, n, 512):
                                groups2.append((mode, c0 + o, min(512, n - o), dst, d0 + o))
                        for (mode, c0, n, dst, d0) in groups2:
                            if mode == 'vd':
                                wv = wload(vres_down[l - 1], 8, 32)
                            else:
                                wv = wload(w_in[l][:, c0:c0 + n], 8, n)
                            if mode in ('fm', 'sg'):
                                for j in range(n // 128):
                                    pa = bank()
                                    for kc in range(8):
                                        S.mm(pa[:, 0:w], wv[:, kc, j * 128:(j + 1) * 128], xn[:, kc, 0:w], start=(kc == 0), stop=(kc == 7), fr=True)
                                    st = nxt('s', stg)
                                    if mode == 'sg':
                                        S.act(st[:, 0:w], pa[:, 0:w], AF.Sigmoid)
                                    else:
                                        S.cp('act', st[:, 0:w], pa[:, 0:w])
                                    S.dma(dst[d0 + j * 128:d0 + (j + 1) * 128, s0:s0 + w], st[:, 0:w], q='act')
                            else:
                                roff = 0 if dst is vsb else 1
                                for tt_ in range(nt):
                                    pa = bank()
                                    for kc in range(8):
                                        S.mm(pa[:, 0:n], xn[:, kc, tt_ * 128:(tt_ + 1) * 128], wv[:, kc, 0:n], start=(kc == 0), stop=(kc == 7), fr=True)
                                    st = nxt('s', stg)
                                    S.cp('act', st[:, 0:n], pa[:, 0:n])
                                    r0 = roff + s0 + tt_ * 128
                                    S.dma(dst[r0:r0 + 128, d0:d0 + n], st[:, 0:n], q='act')
                    else:
                        norm(2 * DEPTH * 8, w, rnd=False, dst=hb)
                        S.dma(outT[:, s0:s0 + w].rearrange("(c p) t -> p c t", p=128), hb[:, :, 0:w], q='act')
                S.barrier()

        def attention():
            with ExitStack() as pes:
                kts = [S.sb("kts%d" % i, [64, LP], pes) for i in range(2)]
                qts = [S.sb("qts%d" % i, [64, LP], pes) for i in range(2)]
                vts = [S.sb("vts%d" % i, [128, NT, 128], pes) for i in range(2)]
                kst = S.sb("kst", [64, LP], pes)
                qst = S.sb("qst", [64, LP], pes)
                vst = S.sb("vst", [128, NT, 64], pes)
                S.memset(vst[:, :, :], 0.0)
                for i in range(2):
                    S.cp('pool', R(vts[i][:, :, 64:128]), vst[:, :, :])
                msk = [S.sb("msk%d" % j, [128, 512], pes) for j in range(4)]
                eb = [S.sb("eb%d" % i, [128, 512], pes) for i in range(2)]
                spb = [S.sb("spb%d" % i, [128, 512], pes) for i in range(3)]
                t1b = [S.sb("t1b%d" % i, [128, 512], pes) for i in range(2)]
                ab = [S.sb("ab%d" % i, [128, 512], pes) for i in range(3)]
                ssum = S.sb("ssum", [128, 512], pes)
                yst = [S.sb("yst%d" % i, [64, 512], pes) for i in range(2)]
                for j in range(4):
                    S.amask(msk[j], 512, 1, -1, -128 * j, ALU.is_gt)
                pz = [PS[0], PS[1]]
                pbt = [PS[2], PS[3]]
                py = [PS[4], PS[5]]
                pairs = []
                for h in range(8):
                    for bi, (s0, w) in enumerate(blocks):
                        nkt = (s0 + w) // 128
                        for kt in range(nkt - 1, -1, -1):
                            pairs.append((h, bi, s0, w, kt, kt == nkt - 1, kt == 0))
                np_ = len(pairs)
                st = {}

                def load_head(h):
                    i = h % 2
                    hs = slice(h * 64, (h + 1) * 64)
                    S.dma(kst[:, :], kT[hs, :])
                    S.dma(qst[:, :], qT[hs, :])
                    S.dma(vst[:, :, :], vsb[:, hs].rearrange("(n p) c -> p n c", p=128))
                    S.cp('pool', R(kts[i][:, :]), kst[:, :])
                    S.cp('pool', R(qts[i][:, :]), qst[:, :])
                    S.cp('pool', R(vts[i][:, :, 0:64]), vst[:, :, :])

                def stageA(i):
                    h, bi, s0, w, kt, first, last = pairs[i]
                    if first and bi == 0:
                        load_head(h)
                    hi = h % 2
                    z = pz[i % 2]
                    S.mm(z[:, 0:w], kts[hi][:, kt * 128:(kt + 1) * 128], qts[hi][:, s0:s0 + w], fr=True)
                    e = eb[i % 2]
                    sp = spb[i % 3]
                    S.act(e[:, 0:w], z[:, 0:w], AF.Exp, scale=0.125)
                    S.act(R(sp[:, 0:w]), e[:, 0:w], AF.Ln, bias=1.0)
                    j = kt - s0 // 128
                    if j >= 0:
                        S.tt('pool', R(sp[:, 0:w]), sp[:, 0:w], msk[j][:, 0:w], ALU.mult)

                def stageB(i):
                    h, bi, s0, w, kt, first, last = pairs[i]
                    z = pz[i % 2]
                    sp = spb[i % 3]
                    bt = pbt[i % 2]
                    S.mm(bt[:, 0:w], triSr[:, :], sp[:, 0:w], start=True, stop=first, fr=True)
                    if not first:
                        S.mm(bt[:, 0:w], onesr[:, :], ssum[:, 0:w], start=False, stop=True, fr=True)
                    if not last:
                        if first:
                            S.cp('dve', R(ssum[:, 0:w]), sp[:, 0:w])
                        else:
                            S.tt('dve', R(ssum[:, 0:w]), ssum[:, 0:w], sp[:, 0:w], ALU.add)
                    t1 = t1b[i % 2]
                    S.stt(t1[:, 0:w], z[:, 0:w], 0.125, sp[:, 0:w], ALU.mult, ALU.subtract)
                    S.tt('dve', t1[:, 0:w], t1[:, 0:w], bt[:, 0:w], ALU.subtract)
                    a = ab[i % 3]
                    S.act(R(a[:, 0:w]), t1[:, 0:w], AF.Exp)
                    j = kt - s0 // 128
                    if j >= 0:
                        S.tt('pool', R(a[:, 0:w]), a[:, 0:w], msk[j][:, 0:w], ALU.mult)

                def stageC(i):
                    h, bi, s0, w, kt, first, last = pairs[i]
                    hi = h % 2
                    y = py[(h * len(blocks) + bi) % 2]
                    a = ab[i % 3]
                    S.mm(y[:, 0:w], vts[hi][:, kt, :], a[:, 0:w], start=first, stop=last, fr=True)
                    if last:
                        ys = yst[(h * len(blocks) + bi) % 2]
                        S.cp('act', ys[:, 0:w], y[0:64, 0:w])
                        S.dma(ysbT[h * 64:(h + 1) * 64, s0:s0 + w], ys[:, 0:w], q='act')

                stageA(0)
                for i in range(np_):
                    if i + 1 < np_:
                        stageA(i + 1)
                    stageB(i)
                    if i >= 1:
                        stageC(i - 1)
                stageC(np_ - 1)
                S.barrier()

        def rwkv(l):
            with ExitStack() as pes:
                def T(name, shape=(128, 512)):
                    return S.sb(name, list(shape), pes)
                vec = T("vec", (128, VEC_LEN))
                cur = T("cur", (128, RW_COLS)); prv = T("prv", (128, RW_COLS)); sh = T("sh", (128, RW_COLS))
                lo = T("lo", (128, 288)); loT = T("loT", (128, 384))
                waup = T("waup"); gup1 = T("gup1"); gup2 = T("gup2", (32, 512)); vup = T("vup", (32, 512))
                xw = T("xw"); sgw = T("sgw"); eta = T("eta"); gsb_ = T("g")
                gam = T("gam"); ginv = T("ginv"); gprev = T("gprev"); gend = T("gend")
                kk = T("kk"); kmod = T("kmod"); bb = T("bb"); vv = T("vv"); tq = T("tq"); tq2 = T("tq2")
                Rt = T("Rt"); At = T("At"); Bt = T("Bt"); Kt = T("Kt"); Bh = T("Bh"); Kh = T("Kh")
                sm = T("sm", (128, 64))
                GC = T("GC", (128, 4))
                vl = T("vl", (128, 96)); vlT = T("vlT", (32, 128))
                FT = [T("FT%d" % p) for p in range(4)]
                AM = [T("AM%d" % h) for h in range(8)]
                Qb = [[T("Q%d_%d" % (g, i)) for i in range(2)] for g in range(2)]
                QTb = [[T("QT%d_%d" % (g, i)) for i in range(2)] for g in range(2)]
                PTb = [T("PT%d" % g) for g in range(2)]
                mask4 = T("mask4"); maskN = T("maskN")
                Hs = [T("H%d" % p, (128, 64)) for p in range(4)]
                Wsb = T("Wsb"); Usb = T("Usb"); ysb_ = T("ysb"); yc = T("yc"); yo = T("yo")
                yT = T("yT", (128, 4, 128))

                S.dma(vec[:, :], vecs[l:l + 1, :].partition_broadcast(128))
                S.dma(waup[:, :], wa_up[l])
                S.dma(gup1[:, :], g_up[l][0:128, :])
                S.dma(gup2[:, :], g_up[l][128:160, :])
                if l > 0:
                    S.dma(vup[:, :], vres_up[l - 1])
                for q in range(4):
                    src = triU if q % 2 == 0 else triUI
                    S.cp('pool', mask4[:, q * 128:(q + 1) * 128], src[:, :])
                    S.cp('pool', maskN[:, q * 128:(q + 1) * 128], triS[:, :])
                for p in range(4):
                    S.memset(Hs[p][:, :], 0.0)

                def h3(v):
                    return v.re("p (h n) -> p h n", n=64)

                def bc(v):
                    return V(v.b, v.a.unsqueeze(2).broadcast_to([128, 8, 64]))

                for t in range(NT):
                    r0 = t * 128
                    S.dma(cur[:, :], rwd[r0 + 1:r0 + 129, :])
                    S.dma(prv[:, :], rwd[r0:r0 + 128, :])
                    S.tt('dve', prv[:, :], prv[:, :], cur[:, :], ALU.subtract)
                    S.tt('dve', prv[:, :], prv[:, :], vec[:, O_MU:O_MU + RW_COLS], ALU.mult)
                    S.tt('dve', sh[:, :], cur[:, :], prv[:, :], ALU.add)
                    r_, kr, vr = sh[:, 0:512], sh[:, 512:1024], sh[:, 1024:1536]
                    S.act(lo[:, 0:64], sh[:, 1536:1600], AF.Tanh)
                    S.cp('pool', lo[:, 64:128], sh[:, 1600:1664])
                    S.act(lo[:, 128:288], sh[:, 1664:1824], AF.Sigmoid)
                    pt = bank()
                    S.tr(pt[:, 0:128], lo[:, 0:128], ident[:, :])
                    S.tr(pt[:, 128:256], lo[:, 128:256], ident[:, :])
                    S.evac(loT[:, 0:256], pt[:, 0:256])
                    pt2 = bank()
                    S.tr(pt2[0:32, 0:128], lo[:, 256:288], ident[:, :])
                    S.evac(loT[0:32, 256:384], pt2[0:32, 0:128])
                    pw = bank()
                    S.mm(pw[:, :], loT[0:64, 0:128], waup[0:64, :])
                    S.tt('dve', xw[:, :], pw[:, :], vec[:, O_W0:O_W0 + 512], ALU.add)
                    S.act(sgw[:, :], xw[:, :], AF.Sigmoid)
                    pa = bank()
                    S.mm(pa[:, :], loT[64:128, 0:128], waup[64:128, :])
                    S.tt('dve', eta[:, :], pa[:, :], vec[:, O_A0:O_A0 + 512], ALU.add)
                    S.act(eta[:, :], eta[:, :], AF.Sigmoid)
                    pg = bank()
                    S.mm(pg[:, :], loT[:, 128:256], gup1[:, :], start=True, stop=False)
                    S.mm(pg[:, :], loT[0:32, 256:384], gup2[:, :], start=False, stop=True)
                    S.evac(gsb_[:, :], pg[:, :])
                    for (msk_, dstb, sc) in ((triUI, gam, -C0), (triUI, ginv, C0), (triU, gprev, -C0), (triS, gend, -C0)):
                        pc = bank()
                        S.mm(pc[:, :], msk_[:, :], sgw[:, :])
                        S.act(dstb[:, :], pc[:, :], AF.Exp, scale=sc)
                    pgc = bank()
                    for p in range(4):
                        S.mm(pgc[:, p:p + 1], sgw[:, p * 128:(p + 1) * 128], ones[:, 0:1])
                    S.act(GC[:, :], pgc[:, 0:4], AF.Exp, scale=-C0)
                    if l == 0:
                        S.cp('pool', vv[:, :], vr)
                        S.dma(vfirst[r0:r0 + 128, :], vv[:, :], q='act')
                    else:
                        S.dma(vl[:, 0:32], vlod[r0 + 1:r0 + 129, :])
                        S.dma(vl[:, 32:64], vlod[r0:r0 + 128, :])
                        S.dma(tq2[:, :], vfirst[r0:r0 + 128, :])
                        S.tt('pool', vl[:, 32:64], vl[:, 32:64], vl[:, 0:32], ALU.subtract)
                        S.tt('dve', vl[:, 32:64], vl[:, 32:64], vec[:, O_VMU:O_VMU + 32], ALU.mult)
                        S.tt('pool', vl[:, 64:96], vl[:, 0:32], vl[:, 32:64], ALU.add)
                        pv = bank()
                        S.tr(pv[0:32, 0:128], vl[:, 64:96], ident[:, :])
                        S.evac(vlT[:, :], pv[0:32, 0:128])
                        pv2 = bank()
                        S.mm(pv2[:, :], vlT[:, :], vup[:, :])
                        S.tt('dve', tq[:, :], pv2[:, :], vec[:, O_VR0:O_VR0 + 512], ALU.add)
                        S.act(tq[:, :], tq[:, :], AF.Sigmoid)
                        S.tt('dve', tq2[:, :], tq2[:, :], vr, ALU.subtract)
                        S.tt('dve', tq2[:, :], tq2[:, :], tq[:, :], ALU.mult)
                        S.tt('dve', vv[:, :], vr, tq2[:, :], ALU.add)
                    S.tt('dve', kk[:, :], kr, vec[:, O_KK:O_KK + 512], ALU.mult)
                    S.act(tq[:, :], kk[:, :], AF.Square)
                    S.red(sm[:, 0:8], h3(tq[:, :]))
                    S.act(sm[:, 0:8], sm[:, 0:8], AF.Sqrt)
                    S.ts('dve', sm[:, 0:8], sm[:, 0:8], 1e-12, ALU.max)
                    S.recip(sm[:, 0:8], sm[:, 0:8])
                    S.tt('dve', h3(kk[:, :]), h3(kk[:, :]), bc(sm[:, 0:8]), ALU.mult)
                    S.stt(tq[:, :], eta[:, :], -1.0, vec[:, O_KA:O_KA + 512], ALU.add, ALU.mult)
                    S.stt(kmod[:, :], tq[:, :], 1.0, kr, ALU.add, ALU.mult)
                    S.tt('dve', bb[:, :], kk[:, :], eta[:, :], ALU.mult)
                    S.tt('pool', tq[:, :], r_, kmod[:, :], ALU.mult)
                    S.tt('dve', tq[:, :], tq[:, :], vec[:, O_RK:O_RK + 512], ALU.mult)
                    S.red(sm[:, 8:16], h3(tq[:, :]))
                    S.tt('dve', Rt[:, :], r_, gam[:, :], ALU.mult)
                    S.stt(At[:, :], kk[:, :], -1.0, gprev[:, :], ALU.mult, ALU.mult)
                    S.tt('pool', Bt[:, :], bb[:, :], ginv[:, :], ALU.mult)
                    S.tt('dve', Kt[:, :], kmod[:, :], ginv[:, :], ALU.mult)
                    S.tt('pool', Bh[:, :], bb[:, :], gend[:, :], ALU.mult)
                    S.tt('dve', Kh[:, :], kmod[:, :], gend[:, :], ALU.mult)
                    for p in range(4):
                        cs = slice(p * 128, (p + 1) * 128)
                        pf = bank()
                        for qi, src in enumerate((At, Rt, Bt, Kt)):
                            S.tr(pf[:, qi * 128:(qi + 1) * 128], src[:, cs], ident[:, :])
                        S.evac(FT[p][:, :], pf[:, :])
                    for g in range(2):
                        pn = bank()
                        for hh in range(4):
                            h = g * 4 + hh
                            p, e = h // 2, (h % 2) * 64
                            f = FT[p]
                            pm = bank()
                            S.mm(pm[:, 0:256], f[e:e + 64, 256:384], f[e:e + 64, 0:256])
                            S.mm(pm[:, 256:512], f[e:e + 64, 384:512], f[e:e + 64, 0:256])
                            S.tt('dve', AM[h][:, :], pm[:, :], mask4[:, :], ALU.mult)
                            S.mm(pn[:, hh * 128:(hh + 1) * 128], f[e:e + 64, 0:128], f[e:e + 64, 256:384])
                        S.tt('dve', R(Qb[g][0][:, :]), pn[:, :], maskN[:, :], ALU.mult)
                        for hh in range(4):
                            h = g * 4 + hh
                            S.cp('pool', R(QTb[g][0][:, hh * 128:(hh + 1) * 128]), AM[h][:, 0:128])
                            S.tt('pool', R(PTb[g][:, hh * 128:(hh + 1) * 128]), AM[h][:, 0:128], ident[:, :], ALU.add)
                    for lev in range(1, 7):
                        for g in range(2):
                            qo, qn = Qb[g][(lev - 1) % 2], Qb[g][lev % 2]
                            qto, qtn = QTb[g][(lev - 1) % 2], QTb[g][lev % 2]
                            pq = bank()
                            for hh in range(4):
                                hs = slice(hh * 128, (hh + 1) * 128)
                                S.mm(pq[:, hs], qto[:, hs], qo[:, hs], fr=True)
                            if lev < 6:
                                pqt = bank()
                                for hh in range(4):
                                    hs = slice(hh * 128, (hh + 1) * 128)
                                    S.mm(pqt[:, hs], qo[:, hs], qto[:, hs], fr=True)
                            S.evac(R(qn[:, :]), pq[:, :])
                            if lev < 6:
                                S.evac(R(qtn[:, :]), pqt[:, :])
                            pp = bank()
                            for hh in range(4):
                                hs = slice(hh * 128, (hh + 1) * 128)
                                S.mm(pp[:, hs], qn[:, hs], PTb[g][:, hs], fr=True)
                            S.tt('dve', R(PTb[g][:, :]), PTb[g][:, :], pp[:, :], ALU.add)
                    pW = bank()
                    for h in range(8):
                        p, e = h // 2, (h % 2) * 64
                        hs = slice(h * 64, (h + 1) * 64)
                        S.mm(pW[:, hs], FT[p][e:e + 64, 0:128], Hs[p][e:e + 64, :], start=True, stop=False)
                        S.mm(pW[:, hs], AM[h][:, 256:384], vv[:, hs], start=False, stop=True)
                    S.evac(Wsb[:, :], pW[:, :])
                    pU = bank()
                    for h in range(8):
                        hs = slice(h * 64, (h + 1) * 64)
                        S.mm(pU[:, hs], PTb[h // 4][:, (h % 4) * 128:(h % 4 + 1) * 128], Wsb[:, hs])
                    S.evac(Usb[:, :], pU[:, :])
                    pY = bank()
                    pH = bank()
                    for h in range(8):
                        p, e = h // 2, (h % 2) * 64
                        hs = slice(h * 64, (h + 1) * 64)
                        S.mm(pY[:, hs], FT[p][e:e + 64, 128:256], Hs[p][e:e + 64, :], start=True, stop=False)
                        S.mm(pY[:, hs], AM[h][:, 384:512], vv[:, hs], start=False, stop=False)
                        S.mm(pY[:, hs], AM[h][:, 128:256], Usb[:, hs], start=False, stop=True)
                    for h in range(8):
                        p, e = h // 2, (h % 2) * 64
                        hs = slice(h * 64, (h + 1) * 64)
                        S.mm(pH[e:e + 64, p * 64:(p + 1) * 64], Bh[:, hs], Usb[:, hs], start=True, stop=False)
                        S.mm(pH[e:e + 64, p * 64:(p + 1) * 64], Kh[:, hs], vv[:, hs], start=False, stop=True)
                    for p in range(4):
                        S.stt(Hs[p][:, :], Hs[p][:, :], GC[:, p:p + 1], pH[:, p * 64:(p + 1) * 64], ALU.mult, ALU.add)
                    S.evac(ysb_[:, :], pY[:, :])
                    S.red(sm[:, 16:24], h3(ysb_[:, :]))
                    S.ts('dve', sm[:, 16:24], sm[:, 16:24], 1.0 / 64, ALU.mult)
                    S.tt('dve', h3(yc[:, :]), h3(ysb_[:, :]), bc(sm[:, 16:24]), ALU.subtract)
                    S.act(tq[:, :], yc[:, :], AF.Square)
                    S.red(sm[:, 24:32], h3(tq[:, :]))
                    S.ts('dve', sm[:, 24:32], sm[:, 24:32], 1.0 / 64, ALU.mult, GN_EPS, ALU.add)
                    S.act(sm[:, 24:32], sm[:, 24:32], AF.Sqrt)
                    S.recip(sm[:, 24:32], sm[:, 24:32])
                    S.tt('dve', h3(yc[:, :]), h3(yc[:, :]), bc(sm[:, 24:32]), ALU.mult)
                    S.tt('dve', yc[:, :], yc[:, :], vec[:, O_LNW:O_LNW + 512], ALU.mult)
                    S.tt('dve', yc[:, :], yc[:, :], vec[:, O_LNB:O_LNB + 512], ALU.add)
                    S.tt('pool', h3(tq[:, :]), h3(vv[:, :]), bc(sm[:, 8:16]), ALU.mult)
                    S.tt('dve', yo[:, :], tq[:, :], yc[:, :], ALU.add)
                    S.tt('dve', yo[:, :], yo[:, :], gsb_[:, :], ALU.mult)
                    po = bank()
                    for p in range(4):
                        S.tr(po[:, p * 128:(p + 1) * 128], yo[:, p * 128:(p + 1) * 128], ident[:, :])
                    S.cp('act', yT[:, :, :].re("p a b -> p (a b)"), po[:, :])
                    S.dma(yrwT[:, r0:r0 + 128].rearrange("(a c) t -> c a t", c=128), yT[:, :, :], q='act')
                S.barrier()

        marks = []
        for l in range(DEPTH + 1):
            rowlocal(l)
            marks.append(('row%d' % l, dict(S.cnt)))
            if l < DEPTH:
                attention()
                marks.append(('att%d' % l, dict(S.cnt)))
                rwkv(l)
                marks.append(('rwkv%d' % l, dict(S.cnt)))
        build.marks = marks
        S.finish()
        print("instructions:", S.nins, {k: S.cnt[k] for k in S.cnt})
    return nc


def make_inputs(LP, DEPTH, x_b, p):
    f = np.float32
    s = x_b.shape[0]
    h0 = np.zeros((LP, D), f)
    h0[:N_META] = p["meta_tokens"]
    h0[N_META:N_META + s] = x_b
    gains = np.concatenate([p["norm_mix"], p["norm_ffn"], p["norm_final"][None]], axis=0)
    gains = np.ascontiguousarray(gains.reshape(2 * DEPTH + 1, 8, 128).transpose(2, 0, 1).reshape(128, -1))
    vec = np.zeros((DEPTH, VEC_LEN), f)
    vec[:, O_MU:O_MU + RW_COLS] = p["mu_rw"]
    vec[:, O_W0:O_W0 + 512] = p["w0"]
    vec[:, O_A0:O_A0 + 512] = p["a0"]
    vec[:, O_KK:O_KK + 512] = p["k_k"]
    vec[:, O_KA:O_KA + 512] = p["k_a"]
    vec[:, O_RK:O_RK + 512] = p["r_k"].reshape(DEPTH, 512)
    vec[:, O_LNW:O_LNW + 512] = p["ln_x_w"]
    vec[:, O_LNB:O_LNB + 512] = p["ln_x_b"]
    if DEPTH > 1:
        vec[1:, O_VMU:O_VMU + 32] = p["vres_mu"]
        vec[1:, O_VR0:O_VR0 + 512] = p["vres0"]
    m = {
        "h0T": np.ascontiguousarray(h0.T),
        "gains": gains.astype(f),
        "vecs": vec,
        "w_in": p["w_in"],
        "wa_up": np.ascontiguousarray(np.concatenate([p["w_up"], p["a_up"]], axis=1)),
        "g_up": p["g_up"],
        "vres_down": p["vres_down"] if DEPTH > 1 else np.zeros((1, D, 32), f),
        "vres_up": p["vres_up"] if DEPTH > 1 else np.zeros((1, 32, C_RW), f),
        "w_sb_out": p["w_sb_out"], "w_rw_out": p["w_rw_out"], "w_out": p["w_out"],
        "w_ffn_in": p["w_ffn_in"], "w_ffn_out": p["w_ffn_out"],
    }
    return {k: np.ascontiguousarray(np.asarray(v, dtype=f)) for k, v in m.items()}


def kernel(**inputs):
    p = {k: np.asarray(v) for k, v in inputs.items()}
    x = p["x"]
    B, S_, _ = x.shape
    DEPTH = p["w_in"].shape[0]
    LP = -(-(N_META + S_) // 128) * 128
    nc = build(LP, DEPTH)
    n_cores = 8
    in_maps = [make_inputs(LP, DEPTH, x[c % B], p) for c in range(n_cores)]
    res = run_bass_kernel_spmd(nc, in_maps, core_ids=list(range(n_cores)))
    out = np.stack([res.results[b]["outT"].T[N_META:N_META + S_] for b in range(B)], axis=0)
    return np.ascontiguousarray(out.astype(np.float32))
```

```python
import numpy as np
import concourse.bass as bass
import concourse.mybir as mybir
from concourse.bass_utils import run_bass_kernel_spmd
from contextlib import ExitStack

F32 = mybir.dt.float32
F32R = mybir.dt.float32r
AF = mybir.ActivationFunctionType
ALU = mybir.AluOpType
AX = mybir.AxisListType
NDS = 12
SAME_ENGINE_SYNC = True

D = 1024
N_META = 16
C_SB = 512
C_RW = 512
RW_COLS = 1824
IN_COLS = 5408
FFN = 2816
RMS_EPS = 1e-6
GN_EPS = 64e-5
C0 = float(np.exp(-0.5))
O_MU, O_W0, O_A0, O_KK, O_KA, O_RK, O_LNW, O_LNB, O_VMU, O_VR0 = 0, 1824, 2336, 2848, 3360, 3872, 4384, 4896, 5408, 5440
VEC_LEN = 5952


class V:
    def __init__(self, b, a):
        self.b = b
        self.a = a

    def __getitem__(self, idx):
        return V(self.b, self.a[idx])

    def re(self, pat, **kw):
        return V(self.b, self.a.rearrange(pat, **kw))


class Buf:
    def __init__(self, name, t=None):
        self.name = name
        self.t = t
        self.w = None
        self.r = {}
        self.alias = []

    def __getitem__(self, idx):
        return V(self, self.t[idx])


def R(v):
    return V(v.b, v.a.bitcast(F32R))


def _bufs(*vs):
    out = []
    for v in vs:
        if isinstance(v, V) and v.b not in out:
            out.append(v.b)
    return out


def _a(v):
    return v.a if isinstance(v, V) else v


class Sched:
    def __init__(self, nc, es):
        self.nc = nc
        self.es = es
        self.eng = {'pe': nc.tensor, 'act': nc.scalar, 'dve': nc.vector, 'pool': nc.gpsimd, 'sp': nc.sync}
        self.sem = {k: es.enter_context(nc.semaphore("s_" + k)) for k in self.eng}
        self.cnt = {k: 0 for k in self.eng}
        self.known = {k: {} for k in self.eng}
        self.dsem = [es.enter_context(nc.semaphore("d%d" % i)) for i in range(NDS)]
        self.dcnt = [0] * NDS
        self.dnext = 0
        self.nins = 0
        self.evq = 0

    def sb(self, name, shape, es=None):
        self.uid = getattr(self, 'uid', 0) + 1
        name = "%s_%d" % (name, self.uid)
        t = (es or self.es).enter_context(self.nc.sbuf_tensor(name, list(shape), F32))
        return Buf(name, t)

    def ps(self, name, es=None):
        t = (es or self.es).enter_context(self.nc.psum_tensor(name, [128, 512], F32))
        return Buf(name, t)

    def _wait(self, e, tok):
        sem, val, key = tok
        if val <= 0 or self.known[e].get(key, 0) >= val:
            return
        if key == e and (e == 'pe' or not SAME_ENGINE_SYNC):
            return
        self.eng[e].wait_ge(sem, val)
        self.known[e][key] = val
        self.nins += 1

    def _deps(self, e, reads, writes):
        for b in reads:
            if b.w is not None:
                self._wait(e, b.w)
        for b in writes:
            for bb in [b] + b.alias:
                if bb.w is not None:
                    self._wait(e, bb.w)
                for t in list(bb.r.values()):
                    self._wait(e, t)

    def _mark(self, tok, reads, writes):
        for b in reads:
            if b not in writes:
                b.r[tok[2]] = tok
        for b in writes:
            b.w = tok
            b.r = {}

    def op(self, e, meth, reads, writes, *args, **kw):
        self._deps(e, reads, writes)
        ins = getattr(self.eng[e], meth)(*args, **kw)
        self.cnt[e] += 1
        ins.then_inc(self.sem[e], 1)
        self.nins += 1
        self._mark((self.sem[e], self.cnt[e], e), reads, writes)

    def dma(self, out, in_, q='sp', **kw):
        reads, writes = _bufs(in_), _bufs(out)
        self._deps(q, reads, writes)
        i = self.dnext
        self.dnext = (i + 1) % NDS
        key = ('d', i)
        self._wait(q, (self.dsem[i], self.dcnt[i], key))
        self.eng[q].dma_start(out=_a(out), in_=_a(in_), **kw).then_inc(self.dsem[i], 16)
        self.dcnt[i] += 16
        self.nins += 1
        self._mark((self.dsem[i], self.dcnt[i], key), reads, writes)

    def barrier(self):
        toks = [(self.sem[k], self.cnt[k], k) for k in self.eng if self.cnt[k] > 0]
        toks += [(self.dsem[i], self.dcnt[i], ('d', i)) for i in range(NDS) if self.dcnt[i] > 0]
        for e in self.eng:
            for t in toks:
                if t[2] != e:
                    self._wait(e, t)

    def finish(self):
        for i in range(NDS):
            self._wait('sp', (self.dsem[i], self.dcnt[i], ('d', i)))
        for k in self.eng:
            if k != 'sp':
                self._wait('sp', (self.sem[k], self.cnt[k], k))

    def mm(self, out, lhsT, rhs, start=True, stop=True, fr=False):
        if fr:
            lhsT, rhs = R(lhsT), R(rhs)
        self.op('pe', 'matmul', _bufs(lhsT, rhs), _bufs(out), out.a, lhsT=lhsT.a, rhs=rhs.a, start=start, stop=stop)

    def tr(self, out, in_, ident):
        self.op('pe', 'transpose', _bufs(in_, ident), _bufs(out), out.a, in_.a, ident.a)

    def act(self, out, in_, func, scale=None, bias=None):
        kw = {}
        if scale is not None:
            kw['scale'] = _a(scale)
        if bias is not None:
            kw['bias'] = _a(bias)
        self.op('act', 'activation', _bufs(in_, scale, bias), _bufs(out), out=out.a, in_=in_.a, func=func, **kw)

    def tt(self, e, out, in0, in1, op):
        self.op(e, 'tensor_tensor', _bufs(in0, in1), _bufs(out), out=out.a, in0=in0.a, in1=in1.a, op=op)

    def ts(self, e, out, in0, s1, op0, s2=None, op1=None):
        kw = dict(out=out.a, in0=in0.a, scalar1=_a(s1), scalar2=_a(s2), op0=op0)
        if op1 is not None:
            kw['op1'] = op1
        self.op(e, 'tensor_scalar', _bufs(in0, s1, s2), _bufs(out), **kw)

    def stt(self, out, in0, scalar, in1, op0, op1):
        self.op('dve', 'scalar_tensor_tensor', _bufs(in0, scalar, in1), _bufs(out), out=out.a, in0=in0.a,
                scalar=_a(scalar), in1=in1.a, op0=op0, op1=op1)

    def cp(self, e, out, in_):
        if e == 'act':
            self.op('act', 'copy', _bufs(in_), _bufs(out), out=out.a, in_=in_.a)
        else:
            self.op(e, 'tensor_copy', _bufs(in_), _bufs(out), out=out.a, in_=in_.a)

    def evac(self, out, in_):
        self.evq ^= 1
        self.cp('act' if self.evq else 'dve', out, in_)

    def red(self, out, in_, op=ALU.add):
        self.op('dve', 'tensor_reduce', _bufs(in_), _bufs(out), out=out.a, in_=in_.a, axis=AX.X, op=op)

    def recip(self, out, in_):
        self.op('dve', 'reciprocal', _bufs(in_), _bufs(out), out=out.a, in_=in_.a)

    def memset(self, out, val):
        self.op('pool', 'memset', [], _bufs(out), out.a, val)

    def amask(self, buf, n, step, cm, base, cmp):
        self.memset(buf[:, 0:n], 1.0)
        self.op('pool', 'affine_select', [buf], [buf], out=buf.t[:, 0:n], in_=buf.t[:, 0:n], pattern=[[step, n]],
                compare_op=cmp, fill=0.0, base=base, channel_multiplier=cm)


def build(LP, DEPTH, debug=False):
    nc = bass.Bass("TRN2", target_bir_lowering=False)
    NT = LP // 128
    blocks = [(s, min(512, LP - s)) for s in range(0, LP, 512)]

    def din(name, shape):
        return nc.dram_tensor(name, list(shape), F32, kind="ExternalInput").ap()

    def dscr(name, shape):
        return nc.dram_tensor(name, list(shape), F32, kind=("ExternalOutput" if debug else "Internal")).ap()

    h0T = din("h0T", [D, LP])
    gains = din("gains", [128, (2 * DEPTH + 1) * 8])
    vecs = din("vecs", [DEPTH, VEC_LEN])
    w_in = din("w_in", [DEPTH, D, IN_COLS])
    wa_up = din("wa_up", [DEPTH, 128, C_RW])
    g_up = din("g_up", [DEPTH, 160, C_RW])
    vres_down = din("vres_down", [max(DEPTH - 1, 1), D, 32])
    vres_up = din("vres_up", [max(DEPTH - 1, 1), 32, C_RW])
    w_sb_out = din("w_sb_out", [DEPTH, C_SB, D])
    w_rw_out = din("w_rw_out", [DEPTH, C_RW, D])
    w_out = din("w_out", [DEPTH, D, D])
    w_ffn_in = din("w_ffn_in", [DEPTH, D, 2 * FFN])
    w_ffn_out = din("w_ffn_out", [DEPTH, FFN, D])
    outT = nc.dram_tensor("outT", [D, LP], F32, kind="ExternalOutput").ap()

    hT = dscr("hT", [D, LP])
    qT = dscr("qT", [C_SB, LP])
    kT = dscr("kT", [C_SB, LP])
    vsb = dscr("vsb", [LP, C_SB])
    rwd = dscr("rwd", [LP + 1, RW_COLS])
    vlod = dscr("vlod", [LP + 1, 32])
    gsbT = dscr("gsbT", [D, LP])
    grwT = dscr("grwT", [D, LP])
    ysbT = dscr("ysbT", [C_SB, LP])
    yrwT = dscr("yrwT", [C_RW, LP])
    vfirst = dscr("vfirst", [LP, C_RW])

    with ExitStack() as es:
        S = Sched(nc, es)
        PS = [S.ps("ps%d" % i) for i in range(8)]
        psn = [0]

        def bank():
            psn[0] = (psn[0] + 1) % 8
            return PS[psn[0]]

        ident = S.sb("ident", [128, 128])
        ones = S.sb("ones", [128, 128])
        triS = S.sb("triS", [128, 128])
        triU = S.sb("triU", [128, 128])
        triUI = S.sb("triUI", [128, 128])
        gn = S.sb("gn", [128, (2 * DEPTH + 1) * 8])
        zrow = S.sb("zrow", [1, 512])
        S.amask(ident, 128, -1, 1, 0, ALU.is_equal)
        S.memset(ones[:, :], 1.0)
        S.amask(triS, 128, -1, 1, 0, ALU.is_gt)
        S.amask(triU, 128, 1, -1, 0, ALU.is_gt)
        S.amask(triUI, 128, 1, -1, 0, ALU.is_ge)
        S.memset(zrow[:, :], 0.0)
        onesr = S.sb("onesr", [128, 128])
        triSr = S.sb("triSr", [128, 128])
        S.cp('pool', R(onesr[:, :]), ones[:, :])
        S.cp('pool', R(triSr[:, :]), triS[:, :])
        S.dma(gn[:, :], gains)
        for o in range(0, RW_COLS, 512):
            n_ = min(512, RW_COLS - o)
            S.dma(rwd[0:1, o:o + n_], zrow[:, 0:n_])
        S.dma(vlod[0:1, :], zrow[:, 0:32])

        def rowlocal(l):
            with ExitStack() as pes:
                hb = S.sb("hb", [128, 8, 512], pes)
                xn = S.sb("xn", [128, 8, 512], pes)
                wbuf = [S.sb("wbuf%d" % i, [128, 5632], pes) for i in range(2)]
                wst = [S.sb("wst%d" % i, [128, 5632], pes) for i in range(2)]
                stg = [S.sb("stg%d" % i, [128, 512], pes) for i in range(3)]
                sq = [S.sb("sq%d" % i, [128, 512], pes) for i in range(2)]
                rs = S.sb("rs", [128, 512], pes)
                if l > 0:
                    big = S.sb("big", [128, 22 * 512], pes).t

                    def sub(name, a, b):
                        return Buf(name, big[:, a * 512:b * 512].rearrange("p (c n) -> p c n", n=512))
                    actb = sub("actb", 0, 22)
                    mg, ysb, yrw = sub("mg", 0, 8), sub("ysb", 8, 12), sub("yrw", 12, 16)
                    yst = S.sb("yst", [128, 4, 512], pes)
                    actb.alias = [mg, ysb, yrw]
                    for b_ in (mg, ysb, yrw):
                        b_.alias = [actb]
                    gt = [S.sb("gt%d" % i, [128, 512], pes) for i in range(4)]
                    tmp = [S.sb("tmp%d" % i, [128, 512], pes) for i in range(4)]
                cnt = {'w': 0, 's': 0, 'g': 0, 't': 0, 'r': 0}

                def nxt(key, lst):
                    cnt[key] += 1
                    return lst[cnt[key] % len(lst)]

                def wload(src2d, kc, n):
                    cnt['w'] += 1
                    i = cnt['w'] % 2
                    vs = wst[i][:, 0:kc * n].re("p (c n) -> p c n", n=n)
                    v = wbuf[i][:, 0:kc * n].re("p (c n) -> p c n", n=n)
                    S.dma(vs, src2d.rearrange("(c p) n -> p c n", p=128))
                    S.cp(('act', 'dve', 'act', 'pool', 'dve')[cnt['w'] % 5], R(v), vs)
                    return v

                def norm(gcol, w, rnd=True, dst=None):
                    dst = dst or xn
                    ps = bank()
                    for c in range(8):
                        s = sq[c % 2]
                        S.act(s[:, 0:w], hb[:, c, 0:w], AF.Square)
                        S.mm(ps[:, 0:w], ones[:, :], s[:, 0:w], start=(c == 0), stop=(c == 7))
                    S.ts('dve', rs[:, 0:w], ps[:, 0:w], 1.0 / D, ALU.mult, RMS_EPS, ALU.add)
                    S.act(rs[:, 0:w], rs[:, 0:w], AF.Sqrt)
                    S.recip(rs[:, 0:w], rs[:, 0:w])
                    for c in range(8):
                        xo = dst[:, c, 0:w]
                        S.stt(R(xo) if rnd else xo, hb[:, c, 0:w], gn[:, gcol + c:gcol + c + 1], rs[:, 0:w], ALU.mult, ALU.mult)

                for (s0, w) in blocks:
                    nt = w // 128
                    S.dma(hb[:, :, 0:w], (h0T if l == 0 else hT)[:, s0:s0 + w].rearrange("(c p) t -> p c t", p=128))
                    if l > 0:
                        lw_ = l - 1
                        S.dma(yst[:, :, 0:w], ysbT[:, s0:s0 + w].rearrange("(c p) t -> p c t", p=128))
                        S.cp('pool', R(ysb[:, :, 0:w]), yst[:, :, 0:w])
                        S.dma(yst[:, :, 0:w], yrwT[:, s0:s0 + w].rearrange("(c p) t -> p c t", p=128))
                        S.cp('pool', R(yrw[:, :, 0:w]), yst[:, :, 0:w])
                        wv = wload(w_sb_out[lw_], 4, 1024)
                        wv2 = wload(w_rw_out[lw_], 4, 1024)
                        for c in range(8):
                            cs = slice(c * 128, (c + 1) * 128)
                            pa = bank()
                            for kc in range(4):
                                S.mm(pa[:, 0:w], wv[:, kc, cs], ysb[:, kc, 0:w], start=(kc == 0), stop=(kc == 3), fr=True)
                            pb = bank()
                            for kc in range(4):
                                S.mm(pb[:, 0:w], wv2[:, kc, cs], yrw[:, kc, 0:w], start=(kc == 0), stop=(kc == 3), fr=True)
                            g1 = nxt('g', gt)
                            S.dma(g1[:, 0:w], gsbT[cs, s0:s0 + w])
                            g2 = nxt('g', gt)
                            S.dma(g2[:, 0:w], grwT[cs, s0:s0 + w])
                            t1 = nxt('t', tmp)
                            S.tt('dve', t1[:, 0:w], g1[:, 0:w], pa[:, 0:w], ALU.mult)
                            t2 = nxt('t', tmp)
                            S.tt('dve', t2[:, 0:w], g2[:, 0:w], pb[:, 0:w], ALU.mult)
                            S.tt('pool', R(mg[:, c, 0:w]), t2[:, 0:w], t1[:, 0:w], ALU.add)
                        for c in range(8):
                            if c % 4 == 0:
                                wv = wload(w_out[lw_][:, c * 128:(c + 4) * 128], 8, 512)
                            pa = bank()
                            for kc in range(8):
                                S.mm(pa[:, 0:w], wv[:, kc, (c % 4) * 128:(c % 4 + 1) * 128], mg[:, kc, 0:w], start=(kc == 0), stop=(kc == 7), fr=True)
                            S.tt('dve', hb[:, c, 0:w], hb[:, c, 0:w], pa[:, 0:w], ALU.add)
                        norm((DEPTH + lw_) * 8, w)
                        for j in range(22):
                            if j % 4 == 0:
                                ng = min(4, 22 - j) * 128
                                wg = wload(w_ffn_in[lw_][:, j * 128:j * 128 + ng], 8, ng)
                                wu = wload(w_ffn_in[lw_][:, FFN + j * 128:FFN + j * 128 + ng], 8, ng)
                            js = slice((j % 4) * 128, (j % 4 + 1) * 128)
                            pg = bank()
                            for kc in range(8):
                                S.mm(pg[:, 0:w], wg[:, kc, js], xn[:, kc, 0:w], start=(kc == 0), stop=(kc == 7), fr=True)
                            pu = bank()
                            for kc in range(8):
                                S.mm(pu[:, 0:w], wu[:, kc, js], xn[:, kc, 0:w], start=(kc == 0), stop=(kc == 7), fr=True)
                            t1 = nxt('t', tmp)
                            S.act(t1[:, 0:w], pg[:, 0:w], AF.Silu)
                            S.tt('dve', R(actb[:, j, 0:w]), t1[:, 0:w], pu[:, 0:w], ALU.mult)
                        for c in range(8):
                            if c % 2 == 0:
                                wv = wload(w_ffn_out[lw_][:, c * 128:(c + 2) * 128], 22, 256)
                            pa = bank()
                            for kc in range(22):
                                S.mm(pa[:, 0:w], wv[:, kc, (c % 2) * 128:(c % 2 + 1) * 128], actb[:, kc, 0:w], start=(kc == 0), stop=(kc == 21), fr=True)
                            S.tt('dve', hb[:, c, 0:w], hb[:, c, 0:w], pa[:, 0:w], ALU.add)
                    if l < DEPTH:
                        S.dma(hT[:, s0:s0 + w].rearrange("(c p) t -> p c t", p=128), hb[:, :, 0:w], q='act')
                        norm(l * 8, w)
                        groups = [('fm', 0, 512, qT, 0), ('fm', 512, 512, kT, 0), ('tm', 1024, 512, vsb, 0),
                                  ('tm', 1536, 512, rwd, 0), ('tm', 2048, 512, rwd, 512), ('tm', 2560, 512, rwd, 1024),
                                  ('tm', 3072, 288, rwd, 1536), ('sg', 3360, 512, gsbT, 0), ('sg', 3872, 512, gsbT, 512),
                                  ('sg', 4384, 512, grwT, 0), ('sg', 4896, 512, grwT, 512)]
                        if l > 0:
                            groups.append(('vd', 0, 32, vlod, 0))
                        groups2 = []
                        for (mode, c0, n, dst, d0) in groups:
                            for o in range(0, n, 512):
                                groups2.append((mode, c0 + o, min(512, n - o), dst, d0 + o))
                        for (mode, c0, n, dst, d0) in groups2:
                            if mode == 'vd':
                                wv = wload(vres_down[l - 1], 8, 32)
                            else:
                                wv = wload(w_in[l][:, c0:c0 + n], 8, n)
                            if mode in ('fm', 'sg'):
                                for j in range(n // 128):
                                    pa = bank()
                                    for kc in range(8):
                                        S.mm(pa[:, 0:w], wv[:, kc, j * 128:(j + 1) * 128], xn[:, kc, 0:w], start=(kc == 0), stop=(kc == 7), fr=True)
                                    st = nxt('s', stg)
                                    if mode == 'sg':
                                        S.act(st[:, 0:w], pa[:, 0:w], AF.Sigmoid)
                                    else:
                                        S.cp('act', st[:, 0:w], pa[:, 0:w])
                                    S.dma(dst[d0 + j * 128:d0 + (j + 1) * 128, s0:s0 + w], st[:, 0:w], q='act')
                            else:
                                roff = 0 if dst is vsb else 1
                                for tt_ in range(nt):
                                    pa = bank()
                                    for kc in range(8):
                                        S.mm(pa[:, 0:n], xn[:, kc, tt_ * 128:(tt_ + 1) * 128], wv[:, kc, 0:n], start=(kc == 0), stop=(kc == 7), fr=True)
                                    st = nxt('s', stg)
                                    S.cp('act', st[:, 0:n], pa[:, 0:n])
                                    r0 = roff + s0 + tt_ * 128
                                    S.dma(dst[r0:r0 + 128, d0:d0 + n], st[:, 0:n], q='act')
                    else:
                        norm(2 * DEPTH * 8, w, rnd=False, dst=hb)
                        S.dma(outT[:, s0:s0 + w].rearrange("(c p) t -> p c t", p=128), hb[:, :, 0:w], q='act')
                S.barrier()

        def attention():
            with ExitStack() as pes:
                kts = [S.sb("kts%d" % i, [64, LP], pes) for i in range(2)]
                qts = [S.sb("qts%d" % i, [64, LP], pes) for i in range(2)]
                vts = [S.sb("vts%d" % i, [128, NT, 128], pes) for i in range(2)]
                kst = S.sb("kst", [64, LP], pes)
                qst = S.sb("qst", [64, LP], pes)
                vst = S.sb("vst", [128, NT, 64], pes)
                S.memset(vst[:, :, :], 0.0)
                for i in range(2):
                    S.cp('pool', R(vts[i][:, :, 64:128]), vst[:, :, :])
                msk = [S.sb("msk%d" % j, [128, 512], pes) for j in range(4)]
                eb = [S.sb("eb%d" % i, [128, 512], pes) for i in range(2)]
                spb = [S.sb("spb%d" % i, [128, 512], pes) for i in range(3)]
                t1b = [S.sb("t1b%d" % i, [128, 512], pes) for i in range(2)]
                ab = [S.sb("ab%d" % i, [128, 512], pes) for i in range(3)]
                ssum2 = [S.sb("ssum%d" % i, [128, 512], pes) for i in range(2)]
                yst = [S.sb("yst%d" % i, [64, 512], pes) for i in range(2)]
                for j in range(4):
                    S.amask(msk[j], 512, 1, -1, -128 * j, ALU.is_gt)
                pz = [PS[0], PS[1]]
                pbt = [PS[2], PS[3]]
                py = [PS[4], PS[5]]
                pairs = []
                for h in range(8):
                    for bi, (s0, w) in enumerate(blocks):
                        nkt = (s0 + w) // 128
                        for kt in range(nkt - 1, -1, -1):
                            pairs.append((h, bi, s0, w, kt, kt == nkt - 1, kt == 0))
                np_ = len(pairs)
                st = {}

                def load_head(h):
                    i = h % 2
                    hs = slice(h * 64, (h + 1) * 64)
                    S.dma(kst[:, :], kT[hs, :])
                    S.dma(qst[:, :], qT[hs, :])
                    S.dma(vst[:, :, :], vsb[:, hs].rearrange("(n p) c -> p n c", p=128))
                    S.cp('pool', R(kts[i][:, :]), kst[:, :])
                    S.cp('pool', R(qts[i][:, :]), qst[:, :])
                    S.cp('pool', R(vts[i][:, :, 0:64]), vst[:, :, :])

                def stageA(i):
                    h, bi, s0, w, kt, first, last = pairs[i]
                    if first and bi == 0:
                        load_head(h)
                    hi = h % 2
                    z = pz[i % 2]
                    S.mm(z[:, 0:w], kts[hi][:, kt * 128:(kt + 1) * 128], qts[hi][:, s0:s0 + w], fr=True)
                    e = eb[i % 2]
                    sp = spb[i % 3]
                    S.act(e[:, 0:w], z[:, 0:w], AF.Exp, scale=0.125)
                    S.act(R(sp[:, 0:w]), e[:, 0:w], AF.Ln, bias=1.0)
                    j = kt - s0 // 128
                    if j >= 0:
                        S.tt('pool', R(sp[:, 0:w]), sp[:, 0:w], msk[j][:, 0:w], ALU.mult)

                def stageB(i):
                    h, bi, s0, w, kt, first, last = pairs[i]
                    z = pz[i % 2]
                    sp = spb[i % 3]
                    bt = pbt[i % 2]
                    S.mm(bt[:, 0:w], triSr[:, :], sp[:, 0:w], start=True, stop=first, fr=True)
                    kpos = (s0 + w) // 128 - 1 - kt
                    scur, snxt = ssum2[kpos % 2], ssum2[(kpos + 1) % 2]
                    if not first:
                        S.mm(bt[:, 0:w], onesr[:, :], scur[:, 0:w], start=False, stop=True, fr=True)
                    if not last:
                        if first:
                            S.cp('pool', R(snxt[:, 0:w]), sp[:, 0:w])
                        else:
                            S.tt('pool', R(snxt[:, 0:w]), scur[:, 0:w], sp[:, 0:w], ALU.add)
                    t1 = t1b[i % 2]
                    S.stt(t1[:, 0:w], z[:, 0:w], 0.125, sp[:, 0:w], ALU.mult, ALU.subtract)
                    S.tt('dve', t1[:, 0:w], t1[:, 0:w], bt[:, 0:w], ALU.subtract)
                    a = ab[i % 3]
                    S.act(R(a[:, 0:w]), t1[:, 0:w], AF.Exp)
                    j = kt - s0 // 128
                    if j >= 0:
                        S.tt('pool', R(a[:, 0:w]), a[:, 0:w], msk[j][:, 0:w], ALU.mult)

                def stageC(i):
                    h, bi, s0, w, kt, first, last = pairs[i]
                    hi = h % 2
                    y = py[(h * len(blocks) + bi) % 2]
                    a = ab[i % 3]
                    S.mm(y[:, 0:w], vts[hi][:, kt, :], a[:, 0:w], start=first, stop=last, fr=True)
                    if last:
                        ys = yst[(h * len(blocks) + bi) % 2]
                        S.cp('act', ys[:, 0:w], y[0:64, 0:w])
                        S.dma(ysbT[h * 64:(h + 1) * 64, s0:s0 + w], ys[:, 0:w], q='act')

                stageA(0)
                for i in range(np_):
                    if i + 1 < np_:
                        stageA(i + 1)
                    stageB(i)
                    if i >= 1:
                        stageC(i - 1)
                stageC(np_ - 1)
                S.barrier()

        def rwkv(l):
            with ExitStack() as pes:
                def T(name, shape=(128, 512)):
                    return S.sb(name, list(shape), pes)
                vec = T("vec", (128, VEC_LEN))
                cur = T("cur", (128, RW_COLS)); prv = T("prv", (128, RW_COLS)); sh = T("sh", (128, RW_COLS))
                lo = T("lo", (128, 288)); loT = T("loT", (128, 384))
                waup = T("waup"); gup1 = T("gup1"); gup2 = T("gup2", (32, 512)); vup = T("vup", (32, 512))
                xw = T("xw"); sgw = T("sgw"); eta = T("eta"); gsb_ = T("g")
                gam = T("gam"); ginv = T("ginv"); gprev = T("gprev"); gend = T("gend")
                kk = T("kk"); kmod = T("kmod"); bb = T("bb"); vv = T("vv"); tq = T("tq"); tq2 = T("tq2")
                Rt = T("Rt"); At = T("At"); Bt = T("Bt"); Kt = T("Kt"); Bh = T("Bh"); Kh = T("Kh")
                sm = T("sm", (128, 64))
                GC = T("GC", (128, 4))
                vl = T("vl", (128, 96)); vlT = T("vlT", (32, 128))
                FT = [T("FT%d" % p) for p in range(4)]
                AM = [T("AM%d" % h) for h in range(8)]
                Qb = [[T("Q%d_%d" % (g, i)) for i in range(2)] for g in range(2)]
                QTb = [[T("QT%d_%d" % (g, i)) for i in range(2)] for g in range(2)]
                PTb = [T("PT%d" % g) for g in range(2)]
                mask4 = T("mask4"); maskN = T("maskN")
                Hs = [T("H%d" % p, (128, 64)) for p in range(4)]
                Wsb = T("Wsb"); Usb = T("Usb"); ysb_ = T("ysb"); yc = T("yc"); yo = T("yo")
                yT = T("yT", (128, 4, 128))

                S.dma(vec[:, :], vecs[l:l + 1, :].partition_broadcast(128))
                S.dma(waup[:, :], wa_up[l])
                S.dma(gup1[:, :], g_up[l][0:128, :])
                S.dma(gup2[:, :], g_up[l][128:160, :])
                if l > 0:
                    S.dma(vup[:, :], vres_up[l - 1])
                for q in range(4):
                    src = triU if q % 2 == 0 else triUI
                    S.cp('pool', mask4[:, q * 128:(q + 1) * 128], src[:, :])
                    S.cp('pool', maskN[:, q * 128:(q + 1) * 128], triS[:, :])
                for p in range(4):
                    S.memset(Hs[p][:, :], 0.0)

                def h3(v):
                    return v.re("p (h n) -> p h n", n=64)

                def bc(v):
                    return V(v.b, v.a.unsqueeze(2).broadcast_to([128, 8, 64]))

                for t in range(NT):
                    r0 = t * 128
                    S.dma(cur[:, :], rwd[r0 + 1:r0 + 129, :])
                    S.dma(prv[:, :], rwd[r0:r0 + 128, :])
                    S.tt('dve', prv[:, :], prv[:, :], cur[:, :], ALU.subtract)
                    S.tt('dve', prv[:, :], prv[:, :], vec[:, O_MU:O_MU + RW_COLS], ALU.mult)
                    S.tt('dve', sh[:, :], cur[:, :], prv[:, :], ALU.add)
                    r_, kr, vr = sh[:, 0:512], sh[:, 512:1024], sh[:, 1024:1536]
                    S.act(lo[:, 0:64], sh[:, 1536:1600], AF.Tanh)
                    S.cp('pool', lo[:, 64:128], sh[:, 1600:1664])
                    S.act(lo[:, 128:288], sh[:, 1664:1824], AF.Sigmoid)
                    pt = bank()
                    S.tr(pt[:, 0:128], lo[:, 0:128], ident[:, :])
                    S.tr(pt[:, 128:256], lo[:, 128:256], ident[:, :])
                    S.evac(loT[:, 0:256], pt[:, 0:256])
                    pt2 = bank()
                    S.tr(pt2[0:32, 0:128], lo[:, 256:288], ident[:, :])
                    S.evac(loT[0:32, 256:384], pt2[0:32, 0:128])
                    pw = bank()
                    S.mm(pw[:, :], loT[0:64, 0:128], waup[0:64, :])
                    S.tt('dve', xw[:, :], pw[:, :], vec[:, O_W0:O_W0 + 512], ALU.add)
                    S.act(sgw[:, :], xw[:, :], AF.Sigmoid)
                    pa = bank()
                    S.mm(pa[:, :], loT[64:128, 0:128], waup[64:128, :])
                    S.tt('dve', eta[:, :], pa[:, :], vec[:, O_A0:O_A0 + 512], ALU.add)
                    S.act(eta[:, :], eta[:, :], AF.Sigmoid)
                    pg = bank()
                    S.mm(pg[:, :], loT[:, 128:256], gup1[:, :], start=True, stop=False)
                    S.mm(pg[:, :], loT[0:32, 256:384], gup2[:, :], start=False, stop=True)
                    S.evac(gsb_[:, :], pg[:, :])
                    for (msk_, dstb, sc) in ((triUI, gam, -C0), (triUI, ginv, C0), (triU, gprev, -C0), (triS, gend, -C0)):
                        pc = bank()
                        S.mm(pc[:, :], msk_[:, :], sgw[:, :])
                        S.act(dstb[:, :], pc[:, :], AF.Exp, scale=sc)
                    pgc = bank()
                    for p in range(4):
                        S.mm(pgc[:, p:p + 1], sgw[:, p * 128:(p + 1) * 128], ones[:, 0:1])
                    S.act(GC[:, :], pgc[:, 0:4], AF.Exp, scale=-C0)
                    if l == 0:
                        S.cp('pool', vv[:, :], vr)
                        S.dma(vfirst[r0:r0 + 128, :], vv[:, :], q='act')
                    else:
                        S.dma(vl[:, 0:32], vlod[r0 + 1:r0 + 129, :])
                        S.dma(vl[:, 32:64], vlod[r0:r0 + 128, :])
                        S.dma(tq2[:, :], vfirst[r0:r0 + 128, :])
                        S.tt('pool', vl[:, 32:64], vl[:, 32:64], vl[:, 0:32], ALU.subtract)
                        S.tt('dve', vl[:, 32:64], vl[:, 32:64], vec[:, O_VMU:O_VMU + 32], ALU.mult)
                        S.tt('pool', vl[:, 64:96], vl[:, 0:32], vl[:, 32:64], ALU.add)
                        pv = bank()
                        S.tr(pv[0:32, 0:128], vl[:, 64:96], ident[:, :])
                        S.evac(vlT[:, :], pv[0:32, 0:128])
                        pv2 = bank()
                        S.mm(pv2[:, :], vlT[:, :], vup[:, :])
                        S.tt('dve', tq[:, :], pv2[:, :], vec[:, O_VR0:O_VR0 + 512], ALU.add)
                        S.act(tq[:, :], tq[:, :], AF.Sigmoid)
                        S.tt('dve', tq2[:, :], tq2[:, :], vr, ALU.subtract)
                        S.tt('dve', tq2[:, :], tq2[:, :], tq[:, :], ALU.mult)
                        S.tt('dve', vv[:, :], vr, tq2[:, :], ALU.add)
                    S.tt('dve', kk[:, :], kr, vec[:, O_KK:O_KK + 512], ALU.mult)
                    S.act(tq[:, :], kk[:, :], AF.Square)
                    S.red(sm[:, 0:8], h3(tq[:, :]))
                    S.act(sm[:, 0:8], sm[:, 0:8], AF.Sqrt)
                    S.ts('dve', sm[:, 0:8], sm[:, 0:8], 1e-12, ALU.max)
                    S.recip(sm[:, 0:8], sm[:, 0:8])
                    S.tt('dve', h3(kk[:, :]), h3(kk[:, :]), bc(sm[:, 0:8]), ALU.mult)
                    S.stt(tq[:, :], eta[:, :], -1.0, vec[:, O_KA:O_KA + 512], ALU.add, ALU.mult)
                    S.stt(kmod[:, :], tq[:, :], 1.0, kr, ALU.add, ALU.mult)
                    S.tt('dve', bb[:, :], kk[:, :], eta[:, :], ALU.mult)
                    S.tt('pool', tq[:, :], r_, kmod[:, :], ALU.mult)
                    S.tt('dve', tq[:, :], tq[:, :], vec[:, O_RK:O_RK + 512], ALU.mult)
                    S.red(sm[:, 8:16], h3(tq[:, :]))
                    S.tt('dve', Rt[:, :], r_, gam[:, :], ALU.mult)
                    S.stt(At[:, :], kk[:, :], -1.0, gprev[:, :], ALU.mult, ALU.mult)
                    S.tt('pool', Bt[:, :], bb[:, :], ginv[:, :], ALU.mult)
                    S.tt('dve', Kt[:, :], kmod[:, :], ginv[:, :], ALU.mult)
                    S.tt('pool', Bh[:, :], bb[:, :], gend[:, :], ALU.mult)
                    S.tt('dve', Kh[:, :], kmod[:, :], gend[:, :], ALU.mult)
                    for p in range(4):
                        cs = slice(p * 128, (p + 1) * 128)
                        pf = bank()
                        for qi, src in enumerate((At, Rt, Bt, Kt)):
                            S.tr(pf[:, qi * 128:(qi + 1) * 128], src[:, cs], ident[:, :])
                        S.evac(FT[p][:, :], pf[:, :])
                    for g in range(2):
                        pn = bank()
                        for hh in range(4):
                            h = g * 4 + hh
                            p, e = h // 2, (h % 2) * 64
                            f = FT[p]
                            pm = bank()
                            S.mm(pm[:, 0:256], f[e:e + 64, 256:384], f[e:e + 64, 0:256])
                            S.mm(pm[:, 256:512], f[e:e + 64, 384:512], f[e:e + 64, 0:256])
                            S.tt('dve', AM[h][:, :], pm[:, :], mask4[:, :], ALU.mult)
                            S.mm(pn[:, hh * 128:(hh + 1) * 128], f[e:e + 64, 0:128], f[e:e + 64, 256:384])
                        S.tt('dve', R(Qb[g][0][:, :]), pn[:, :], maskN[:, :], ALU.mult)
                        for hh in range(4):
                            h = g * 4 + hh
                            S.cp('pool', R(QTb[g][0][:, hh * 128:(hh + 1) * 128]), AM[h][:, 0:128])
                            S.tt('pool', R(PTb[g][:, hh * 128:(hh + 1) * 128]), AM[h][:, 0:128], ident[:, :], ALU.add)
                    for lev in range(1, 7):
                        for g in range(2):
                            qo, qn = Qb[g][(lev - 1) % 2], Qb[g][lev % 2]
                            qto, qtn = QTb[g][(lev - 1) % 2], QTb[g][lev % 2]
                            pq = bank()
                            for hh in range(4):
                                hs = slice(hh * 128, (hh + 1) * 128)
                                S.mm(pq[:, hs], qto[:, hs], qo[:, hs], fr=True)
                            if lev < 6:
                                pqt = bank()
                                for hh in range(4):
                                    hs = slice(hh * 128, (hh + 1) * 128)
                                    S.mm(pqt[:, hs], qo[:, hs], qto[:, hs], fr=True)
                            S.evac(R(qn[:, :]), pq[:, :])
                            if lev < 6:
                                S.evac(R(qtn[:, :]), pqt[:, :])
                            pp = bank()
                            for hh in range(4):
                                hs = slice(hh * 128, (hh + 1) * 128)
                                S.mm(pp[:, hs], qn[:, hs], PTb[g][:, hs], fr=True)
                            S.tt('dve', R(PTb[g][:, :]), PTb[g][:, :], pp[:, :], ALU.add)
                    pW = bank()
                    for h in range(8):
                        p, e = h // 2, (h % 2) * 64
                        hs = slice(h * 64, (h + 1) * 64)
                        S.mm(pW[:, hs], FT[p][e:e + 64, 0:128], Hs[p][e:e + 64, :], start=True, stop=False)
                        S.mm(pW[:, hs], AM[h][:, 256:384], vv[:, hs], start=False, stop=True)
                    S.evac(Wsb[:, :], pW[:, :])
                    pU = bank()
                    for h in range(8):
                        hs = slice(h * 64, (h + 1) * 64)
                        S.mm(pU[:, hs], PTb[h // 4][:, (h % 4) * 128:(h % 4 + 1) * 128], Wsb[:, hs])
                    S.evac(Usb[:, :], pU[:, :])
                    pY = bank()
                    pH = bank()
                    for h in range(8):
                        p, e = h // 2, (h % 2) * 64
                        hs = slice(h * 64, (h + 1) * 64)
                        S.mm(pY[:, hs], FT[p][e:e + 64, 128:256], Hs[p][e:e + 64, :], start=True, stop=False)
                        S.mm(pY[:, hs], AM[h][:, 384:512], vv[:, hs], start=False, stop=False)
                        S.mm(pY[:, hs], AM[h][:, 128:256], Usb[:, hs], start=False, stop=True)
                    for h in range(8):
                        p, e = h // 2, (h % 2) * 64
                        hs = slice(h * 64, (h + 1) * 64)
                        S.mm(pH[e:e + 64, p * 64:(p + 1) * 64], Bh[:, hs], Usb[:, hs], start=True, stop=False)
                        S.mm(pH[e:e + 64, p * 64:(p + 1) * 64], Kh[:, hs], vv[:, hs], start=False, stop=True)
                    for p in range(4):
                        S.stt(Hs[p][:, :], Hs[p][:, :], GC[:, p:p + 1], pH[:, p * 64:(p + 1) * 64], ALU.mult, ALU.add)
                    S.evac(ysb_[:, :], pY[:, :])
                    S.red(sm[:, 16:24], h3(ysb_[:, :]))
                    S.ts('dve', sm[:, 16:24], sm[:, 16:24], 1.0 / 64, ALU.mult)
                    S.tt('dve', h3(yc[:, :]), h3(ysb_[:, :]), bc(sm[:, 16:24]), ALU.subtract)
                    S.act(tq[:, :], yc[:, :], AF.Square)
                    S.red(sm[:, 24:32], h3(tq[:, :]))
                    S.ts('dve', sm[:, 24:32], sm[:, 24:32], 1.0 / 64, ALU.mult, GN_EPS, ALU.add)
                    S.act(sm[:, 24:32], sm[:, 24:32], AF.Sqrt)
                    S.recip(sm[:, 24:32], sm[:, 24:32])
                    S.tt('dve', h3(yc[:, :]), h3(yc[:, :]), bc(sm[:, 24:32]), ALU.mult)
                    S.tt('dve', yc[:, :], yc[:, :], vec[:, O_LNW:O_LNW + 512], ALU.mult)
                    S.tt('dve', yc[:, :], yc[:, :], vec[:, O_LNB:O_LNB + 512], ALU.add)
                    S.tt('pool', h3(tq[:, :]), h3(vv[:, :]), bc(sm[:, 8:16]), ALU.mult)
                    S.tt('dve', yo[:, :], tq[:, :], yc[:, :], ALU.add)
                    S.tt('dve', yo[:, :], yo[:, :], gsb_[:, :], ALU.mult)
                    po = bank()
                    for p in range(4):
                        S.tr(po[:, p * 128:(p + 1) * 128], yo[:, p * 128:(p + 1) * 128], ident[:, :])
                    S.cp('act', yT[:, :, :].re("p a b -> p (a b)"), po[:, :])
                    S.dma(yrwT[:, r0:r0 + 128].rearrange("(a c) t -> c a t", c=128), yT[:, :, :], q='act')
                S.barrier()

        marks = []
        for l in range(DEPTH + 1):
            rowlocal(l)
            marks.append(('row%d' % l, dict(S.cnt)))
            if l < DEPTH:
                attention()
                marks.append(('att%d' % l, dict(S.cnt)))
                rwkv(l)
                marks.append(('rwkv%d' % l, dict(S.cnt)))
        build.marks = marks
        S.finish()
        print("instructions:", S.nins, {k: S.cnt[k] for k in S.cnt})
    return nc


def make_inputs(LP, DEPTH, x_b, p):
    f = np.float32
    s = x_b.shape[0]
    h0 = np.zeros((LP, D), f)
    h0[:N_META] = p["meta_tokens"]
    h0[N_META:N_META + s] = x_b
    gains = np.concatenate([p["norm_mix"], p["norm_ffn"], p["norm_final"][None]], axis=0)
    gains = np.ascontiguousarray(gains.reshape(2 * DEPTH + 1, 8, 128).transpose(2, 0, 1).reshape(128, -1))
    vec = np.zeros((DEPTH, VEC_LEN), f)
    vec[:, O_MU:O_MU + RW_COLS] = p["mu_rw"]
    vec[:, O_W0:O_W0 + 512] = p["w0"]
    vec[:, O_A0:O_A0 + 512] = p["a0"]
    vec[:, O_KK:O_KK + 512] = p["k_k"]
    vec[:, O_KA:O_KA + 512] = p["k_a"]
    vec[:, O_RK:O_RK + 512] = p["r_k"].reshape(DEPTH, 512)
    vec[:, O_LNW:O_LNW + 512] = p["ln_x_w"]
    vec[:, O_LNB:O_LNB + 512] = p["ln_x_b"]
    if DEPTH > 1:
        vec[1:, O_VMU:O_VMU + 32] = p["vres_mu"]
        vec[1:, O_VR0:O_VR0 + 512] = p["vres0"]
    m = {
        "h0T": np.ascontiguousarray(h0.T),
        "gains": gains.astype(f),
        "vecs": vec,
        "w_in": p["w_in"],
        "wa_up": np.ascontiguousarray(np.concatenate([p["w_up"], p["a_up"]], axis=1)),
        "g_up": p["g_up"],
        "vres_down": p["vres_down"] if DEPTH > 1 else np.zeros((1, D, 32), f),
        "vres_up": p["vres_up"] if DEPTH > 1 else np.zeros((1, 32, C_RW), f),
        "w_sb_out": p["w_sb_out"], "w_rw_out": p["w_rw_out"], "w_out": p["w_out"],
        "w_ffn_in": p["w_ffn_in"], "w_ffn_out": p["w_ffn_out"],
    }
    return {k: np.ascontiguousarray(np.asarray(v, dtype=f)) for k, v in m.items()}


def kernel(**inputs):
    p = {k: np.asarray(v) for k, v in inputs.items()}
    x = p["x"]
    B, S_, _ = x.shape
    DEPTH = p["w_in"].shape[0]
    LP = -(-(N_META + S_) // 128) * 128
    nc = build(LP, DEPTH)
    n_cores = 8
    in_maps = [make_inputs(LP, DEPTH, x[c % B], p) for c in range(n_cores)]
    res = run_bass_kernel_spmd(nc, in_maps, core_ids=list(range(n_cores)))
    out = np.stack([res.results[b]["outT"].T[N_META:N_META + S_] for b in range(B)], axis=0)
    return np.ascontiguousarray(out.astype(np.float32))
```
